# Optimizing a Trainium2 kernel written in Bass

```python
import math
import jax, jax.numpy as jnp
from jax import lax
import numpy as np

D_MODEL = 2048
BATCH = 4
SEQ = 2048
DEPTH = 4

N_MEM = 256
BRANCH_W = 1024
N_BRANCH = 3
CHUNK = 128
SG_GROUPS = 8
SG_GROUP_W = BRANCH_W // SG_GROUPS
SSM_GROUP_W = 16
SSM_GROUPS = BRANCH_W // SSM_GROUP_W
SSM_STATE = 64
XA_HEADS = 4
XA_HEAD_DIM = BRANCH_W // XA_HEADS
N_IN = 7 * BRANCH_W + N_BRANCH * D_MODEL
EPS = 1e-6
DT_MIN = 1e-3
DT_MAX = 1e-1
EIG_CLIP = 1e-4

kernel_name = 'hybrid_sg_s5_memxattn_trunk'


def rms_norm(x, g):
    xf = x.astype(jnp.float32)
    y = xf * lax.rsqrt(jnp.mean(xf * xf, axis=-1, keepdims=True) + EPS)
    return (y * g.astype(jnp.float32)).astype(x.dtype)


def layer_norm(x, g, b):
    xf = x.astype(jnp.float32)
    mu = jnp.mean(xf, axis=-1, keepdims=True)
    xc = xf - mu
    y = xc * lax.rsqrt(jnp.mean(xc * xc, axis=-1, keepdims=True) + EPS)
    return (y * g.astype(jnp.float32) + b.astype(jnp.float32)).astype(x.dtype)


def spatial_gating(u, v, ln_g, ln_b, w_s, b_s):
    bsz, seq, _ = v.shape
    n_chunks = seq // CHUNK
    vn = layer_norm(v, ln_g, ln_b).reshape(bsz, n_chunks, CHUNK, SG_GROUPS, SG_GROUP_W)
    causal = jnp.tril(jnp.ones((CHUNK, CHUNK), dtype=bool))
    w = jnp.where(causal[None], w_s, 0).astype(vn.dtype)
    z = jnp.einsum('gts,bcsgd->bctgd', w, vn) + b_s.T[None, None, :, :, None].astype(vn.dtype)
    return u * z.reshape(bsz, seq, BRANCH_W)


def s5_branch(u, a_re, a_im, log_dt, b_re, b_im, c_re, c_im, d, glu_w, glu_b):
    bsz, seq, _ = u.shape
    f32 = jnp.float32
    uf = u.astype(f32).reshape(bsz, seq, SSM_GROUPS, SSM_GROUP_W)
    lam = lax.complex(jnp.minimum(a_re.astype(f32), -EIG_CLIP), a_im.astype(f32))
    dt = jnp.exp(log_dt.astype(f32))[:, None]
    lam_bar = jnp.exp(lam * dt)
    b_mat = lax.complex(b_re.astype(f32), b_im.astype(f32))
    b_bar = ((lam_bar - 1) / lam)[:, :, None] * b_mat
    c_mat = lax.complex(c_re.astype(f32), c_im.astype(f32))
    bu = jnp.einsum('bsgh,gph->bsgp', uf.astype(jnp.complex64), b_bar)
    a_el = jnp.broadcast_to(lam_bar, bu.shape)

    def combine(left, right):
        a_l, s_l = left
        a_r, s_r = right
        return a_r * a_l, a_r * s_l + s_r

    _, states = lax.associative_scan(combine, (a_el, bu), axis=1)
    y = jnp.einsum('bsgp,ghp->bsgh', states, c_mat).real + d.astype(f32) * uf
    y = jax.nn.gelu(y.reshape(bsz, seq, BRANCH_W)).astype(u.dtype)
    return y * jax.nn.sigmoid(y @ glu_w + glu_b)


def memory_cross_attention(q, mem_n, w_k, w_v):
    bsz, seq, _ = q.shape
    n_mem = mem_n.shape[1]
    qh = q.reshape(bsz, seq, XA_HEADS, XA_HEAD_DIM)
    k = (mem_n @ w_k).reshape(bsz, n_mem, XA_HEADS, XA_HEAD_DIM)
    v = (mem_n @ w_v).reshape(bsz, n_mem, XA_HEADS, XA_HEAD_DIM)
    scores = jnp.einsum('bshd,bmhd->bhsm', qh, k).astype(jnp.float32) * (XA_HEAD_DIM ** -0.5)
    p = jax.nn.softmax(scores, axis=-1).astype(v.dtype)
    o = jnp.einsum('bhsm,bmhd->bshd', p, v)
    return o.reshape(bsz, seq, BRANCH_W)


def hybrid_layer(x, mem, pre_g, post_g, mem_g, w_in, sg_ln_g, sg_ln_b, sg_w, sg_b,
                 a_re, a_im, log_dt, b_re, b_im, c_re, c_im, d, glu_w, glu_b,
                 xa_wk, xa_wv, w_branch, w_out):
    bsz, seq, _ = x.shape
    h = rms_norm(x, pre_g)
    proj = h @ w_in
    splits = [BRANCH_W * i for i in range(1, 8)]
    a_u, a_v, a_g, b_x, b_g, x_q, x_g, gate_logits = jnp.split(proj, splits, axis=-1)
    y_a = spatial_gating(a_u, a_v, sg_ln_g, sg_ln_b, sg_w, sg_b) * jax.nn.silu(a_g)
    y_b = s5_branch(b_x, a_re, a_im, log_dt, b_re, b_im, c_re, c_im, d, glu_w, glu_b) * jax.nn.silu(b_g)
    y_x = memory_cross_attention(x_q, rms_norm(mem, mem_g), xa_wk, xa_wv) * jax.nn.silu(x_g)
    ys = jnp.stack([y_a, y_b, y_x], axis=2)
    branch = jnp.einsum('bsnc,ncd->bsnd', ys, w_branch)
    gates = jax.nn.sigmoid(gate_logits.reshape(bsz, seq, N_BRANCH, D_MODEL))
    merged = jnp.sum(gates * branch, axis=2)
    out = merged @ w_out
    return x + rms_norm(out, post_g)


def setup_inputs(seed: int = 0) -> dict:
    key = jax.random.key(seed)
    ks = jax.random.split(key, 26)
    f32 = jnp.float32

    def nrm(k, shape, scale):
        return jax.random.normal(k, shape, f32) * scale

    L, D, W = DEPTH, D_MODEL, BRANCH_W
    G, H, P = SSM_GROUPS, SSM_GROUP_W, SSM_STATE
    x = nrm(ks[0], (BATCH, SEQ, D), 1.0)
    mem = nrm(ks[1], (BATCH, N_MEM, D), 1.0)
    pre_norm_g = 1.0 + nrm(ks[2], (L, D), 0.05)
    post_norm_g = 1.0 + nrm(ks[3], (L, D), 0.05)
    mem_norm_g = 1.0 + nrm(ks[4], (L, D), 0.05)
    w_in = nrm(ks[5], (L, D, N_IN), D ** -0.5)
    sg_ln_g = 1.0 + nrm(ks[6], (L, W), 0.05)
    sg_ln_b = nrm(ks[7], (L, W), 0.01)
    sg_w = nrm(ks[8], (L, SG_GROUPS, CHUNK, CHUNK), 0.5 * CHUNK ** -0.5)
    sg_b = 1.0 + nrm(ks[9], (L, SG_GROUPS, CHUNK), 0.1)
    ssm_a_re = -0.5 + nrm(ks[10], (L, G, P), 0.01)
    ssm_a_im = jnp.broadcast_to(math.pi * jnp.arange(P, dtype=f32), (L, G, P))
    ssm_log_dt = jax.random.uniform(ks[11], (L, G), f32, math.log(DT_MIN), math.log(DT_MAX))
    ssm_b_re = nrm(ks[12], (L, G, P, H), (2 * H) ** -0.5)
    ssm_b_im = nrm(ks[13], (L, G, P, H), (2 * H) ** -0.5)
    ssm_c_re = nrm(ks[14], (L, G, H, P), P ** -0.5)
    ssm_c_im = nrm(ks[15], (L, G, H, P), P ** -0.5)
    ssm_d = nrm(ks[16], (L, G, H), 0.5)
    glu_w = nrm(ks[17], (L, W, W), W ** -0.5)
    glu_b = nrm(ks[18], (L, W), 0.01)
    xa_wk = nrm(ks[19], (L, D, W), D ** -0.5)
    xa_wv = nrm(ks[20], (L, D, W), D ** -0.5)
    w_branch = nrm(ks[21], (L, N_BRANCH, W, D), W ** -0.5)
    w_out = nrm(ks[22], (L, D, D), D ** -0.5)
    return {'x': x, 'mem': mem, 'pre_norm_g': pre_norm_g, 'post_norm_g': post_norm_g,
            'mem_norm_g': mem_norm_g, 'w_in': w_in, 'sg_ln_g': sg_ln_g, 'sg_ln_b': sg_ln_b,
            'sg_w': sg_w, 'sg_b': sg_b, 'ssm_a_re': ssm_a_re, 'ssm_a_im': ssm_a_im,
            'ssm_log_dt': ssm_log_dt, 'ssm_b_re': ssm_b_re, 'ssm_b_im': ssm_b_im,
            'ssm_c_re': ssm_c_re, 'ssm_c_im': ssm_c_im, 'ssm_d': ssm_d, 'glu_w': glu_w,
            'glu_b': glu_b, 'xa_wk': xa_wk, 'xa_wv': xa_wv, 'w_branch': w_branch, 'w_out': w_out}


def reference(x, mem, pre_norm_g, post_norm_g, mem_norm_g, w_in, sg_ln_g, sg_ln_b, sg_w, sg_b,
              ssm_a_re, ssm_a_im, ssm_log_dt, ssm_b_re, ssm_b_im, ssm_c_re, ssm_c_im, ssm_d,
              glu_w, glu_b, xa_wk, xa_wv, w_branch, w_out):
    for l in range(DEPTH):
        x = hybrid_layer(x, mem, pre_norm_g[l], post_norm_g[l], mem_norm_g[l], w_in[l],
                         sg_ln_g[l], sg_ln_b[l], sg_w[l], sg_b[l],
                         ssm_a_re[l], ssm_a_im[l], ssm_log_dt[l], ssm_b_re[l], ssm_b_im[l],
                         ssm_c_re[l], ssm_c_im[l], ssm_d[l], glu_w[l], glu_b[l],
                         xa_wk[l], xa_wv[l], w_branch[l], w_out[l])
    return x
```

```python
import math
from contextlib import ExitStack
import numpy as np
import concourse.bass as bass
import concourse.mybir as mybir
from concourse.bass_utils import run_bass_kernel_spmd

F32 = mybir.dt.float32
BF16 = mybir.dt.bfloat16
AF = mybir.ActivationFunctionType
ALU = mybir.AluOpType
PI = math.pi

D = 2048
W = 1024
NIN = 13312
NMEM = 256
NT = 512
DEPTH = 4
EPS = 1e-6
ENGS = ("pe", "act", "dve", "pool", "sp")


DEBUG_PHASE = 0


class _Stop(Exception):
    pass


class Buf:
    __slots__ = ("w", "r", "const")

    def __init__(self, const=False):
        self.w = None
        self.r = {}
        self.const = const


class Prog:
    def __init__(self):
        self.ops = {e: [] for e in ENGS}
        self.cnt = {e: 0 for e in ENGS}
        self.waited = {e: {} for e in ENGS}
        self.dsem = {}

    def _deps(self, eng, reads, writes):
        deps = []
        for b in reads:
            if b.w is not None:
                deps.append(b.w)
        for b in writes:
            if b.w is not None:
                deps.append(b.w)
            deps.extend(b.r.items())
        out = []
        wt = self.waited[eng]
        for (s, v) in deps:
            if wt.get(s, 0) >= v:
                continue
            wt[s] = v
            out.append((s, v))
        return out

    def _reg(self, tk, reads, writes):
        for b in reads:
            if not b.const:
                if b.r.get(tk[0], 0) < tk[1]:
                    b.r[tk[0]] = tk[1]
        for b in writes:
            b.w = tk
            b.r = {}

    def op(self, eng, fn, reads=(), writes=(), signal=True):
        waits = self._deps(eng, reads, writes)
        tk = None
        if signal:
            self.cnt[eng] += 1
            tk = (eng, self.cnt[eng])
            self._reg(tk, reads, writes)
        self.ops[eng].append((waits, fn, eng if signal else None, 1))
        return tk

    def dma(self, semname, fn, reads=(), writes=(), queue="sp"):
        waits = self._deps(queue, reads, writes)
        self.dsem[semname] = self.dsem.get(semname, 0) + 16
        tk = (semname, self.dsem[semname])
        self._reg(tk, reads, writes)
        self.ops[queue].append((waits, fn, semname, 16))
        return tk


def build_program(ntiles, nlayers):
    ntok = ntiles * NT
    nc = bass.Bass("TRN2", target_bir_lowering=False)
    P = Prog()

    def din(name, shape):
        return nc.dram_tensor(name, list(shape), F32, kind="ExternalInput").ap()

    xT = din("xT", [D, ntok])
    memT = din("memT", [D, NMEM])
    outT = nc.dram_tensor("outT", [D, ntok], F32, kind="ExternalOutput").ap()
    if nlayers > 1:
        xscr = [nc.dram_tensor("xscr%d" % i, [D, ntok], F32, kind="Internal").ap() for i in range(2)]
    w_in = din("w_in", [nlayers, D, NIN])
    glu_w = din("glu_w", [nlayers, W, W])
    wk = din("wk", [nlayers, D, W])
    wv = din("wv", [nlayers, D, W])
    w_br = din("w_br", [nlayers, 3, W, D])
    w_out = din("w_out", [nlayers, D, D])
    cols_d = din("cols", [nlayers, 128, 80])
    sgwT_d = din("sgwT", [nlayers, 128, 8 * 128])
    sgb_d = din("sgb", [nlayers, 128, 8 * 128])
    s5a_d = din("s5a", [nlayers, 128, 3 * 32])
    bpad_d = din("bpad", [nlayers, 128, 2 * 32 * 128])
    cpad_d = din("cpad", [nlayers, 128, 2 * 32 * 128])
    consts_d = din("consts", [128, 128 + 128 + 32 + 16])

    es = ExitStack()

    def sb(name, shape, dt=F32):
        return es.enter_context(nc.sbuf_tensor("sb_" + name, list(shape), dt))

    def sem(name):
        return es.enter_context(nc.semaphore(name))

    with es:
        consts = sb("consts", [128, 304])
        ident_bf = sb("ident_bf", [128, 128], BF16)
        ones_bf = sb("ones_bf", [128, 128], BF16)
        cols = sb("cols", [128, 80])
        sgwT_bf = sb("sgwT_bf", [128, 8 * 128], BF16)
        sgb = sb("sgb", [128, 8 * 128])
        bpad_bf = sb("bpad_bf", [128, 2 * 32 * 128], BF16)
        cp_bf = sb("cp_bf", [128, 2 * 32 * 128], BF16)
        s5a = sb("s5a", [128, 96])
        s5s = sb("s5s", [128, 16 * 32])
        tabA = sb("tabA", [128, 2 * 32 * 32])
        tabB = sb("tabB", [128, 2 * 32 * 16])
        carry = sb("carry", [128, 2 * 32])
        kT_bf = sb("kT_bf", [128, 8 * 256], BF16)
        vtok_bf = sb("vtok_bf", [128, 2 * 1024], BF16)
        NSLOT = 2
        stage = [sb("stage%d" % i, [128, 16 * 128]) for i in range(NSLOT)]
        wbf = [sb("wbf%d" % i, [128, 16 * 128], BF16) for i in range(NSLOT)]
        hT = sb("hT", [128, 16 * NT], BF16)
        ya = sb("ya", [128, 8 * NT], BF16)
        yb = sb("yb", [128, 8 * NT], BF16)
        yx = sb("yx", [128, 8 * NT], BF16)
        SF = sb("SF", [128, 8192])
        SB = sb("SB", [128, 16384], BF16)
        rstd = sb("rstd", [128, NT])
        tA = sb("tA", [128, NT])
        tB = sb("tB", [128, NT])
        tC = sb("tC", [128, NT])
        tD = sb("tD", [128, NT])
        tE = sb("tE", [128, NT])
        tF = sb("tF", [128, NT])
        psf = [es.enter_context(nc.psum_tensor("ps%d" % i, [128, NT], F32)) for i in range(7)]
        pst = es.enter_context(nc.psum_tensor("pst", [128, 1024], BF16))

        B = {}

        def bf(name, const=False):
            if name not in B:
                B[name] = Buf(const)
            return B[name]

        bank = [bf("bank%d" % i) for i in range(8)]

        def alias(dst, src):
            for d in dst:
                b = bf(d)
                for s_ in src:
                    o = bf(s_)
                    for k_, v_ in o.r.items():
                        b.r[k_] = max(b.r.get(k_, 0), v_)
                    if o.w is not None:
                        b.r[o.w[0]] = max(b.r.get(o.w[0], 0), o.w[1])

        HT = ["hT%d" % k for k in range(16)]
        S5T = ["gre", "gim", "Gre", "Gim", "m1", "m2", "p1", "p2"]

        def dve(fn, r=(), w=()):
            return P.op("dve", fn, [bf(x) for x in r], [bf(x) for x in w])

        def pool(fn, r=(), w=()):
            return P.op("pool", fn, [bf(x) for x in r], [bf(x) for x in w])

        def act(fn, r=(), w=()):
            return P.op("act", fn, [bf(x) for x in r], [bf(x) for x in w])

        def ld(semname, out_ap, in_ap, w=(), r=()):
            return P.dma(semname, lambda e: e.dma_start(out=out_ap, in_=in_ap),
                         [bf(x) for x in r], [bf(x) for x in w])

        def A_act(out, in_, func, bias=None, scale=1.0):
            if bias is None:
                return lambda e: e.activation(out=out, in_=in_, func=func, scale=scale)
            return lambda e: e.activation(out=out, in_=in_, func=func, bias=bias, scale=scale)

        def A_tt(out, in0, in1, op):
            return lambda e: e.tensor_tensor(out=out, in0=in0, in1=in1, op=op)

        def A_ts(out, in0, s1, s2, op0, op1=None):
            if op1 is None:
                return lambda e: e.tensor_scalar(out=out, in0=in0, scalar1=s1, scalar2=None, op0=op0)
            return lambda e: e.tensor_scalar(out=out, in0=in0, scalar1=s1, scalar2=s2, op0=op0, op1=op1)

        def A_stt(out, in0, scalar, in1, op0, op1):
            return lambda e: e.scalar_tensor_tensor(out=out, in0=in0, scalar=scalar, in1=in1, op0=op0, op1=op1)

        def A_cp(out, in_):
            return lambda e: e.tensor_copy(out=out, in_=in_)

        def mm_group(bk, out_ap, items, extra_r=()):
            n = len(items)
            allr = set(extra_r)
            for it in items:
                allr.update(it[2])
            tk = None
            for i, (l, r_, names) in enumerate(items):
                st, sp_ = (i == 0), (i == n - 1)

                def fn(e, l=l, r_=r_, st=st, sp_=sp_):
                    return e.matmul(out_ap, l, r_, start=st, stop=sp_)
                if sp_:
                    tk = P.op("pe", fn, [bf(x) for x in allr], [bank[bk]], signal=True)
                else:
                    P.op("pe", fn, [bf(x) for x in names], [bank[bk]] if st else [], signal=False)
            return tk

        wctr = [0]

        def wblock4(src_ap, ncol=512, cast="act"):
            s_ = wctr[0] % NSLOT
            wctr[0] += 1
            n = 4 * ncol
            stv = stage[s_][:, 0:n].rearrange("p (k c) -> p k c", c=ncol)
            ld("wst%d" % s_, stv, src_ap.rearrange("(k p) c -> p k c", p=128), w=["stage%d" % s_])
            if cast == "act":
                act(A_act(wbf[s_][:, 0:n], stage[s_][:, 0:n], AF.Copy), r=["stage%d" % s_],
                    w=["wbfA%d" % s_, "wbfB%d" % s_])
            else:
                dve(A_cp(wbf[s_][:, 0:n], stage[s_][:, 0:n]), r=["stage%d" % s_],
                    w=["wbfA%d" % s_, "wbfB%d" % s_])
            return s_

        def proj_group(src_ap, K, rhs_fn, rhs_names, ncols=NT, cast="dve"):
            nkb = K // 4
            for kb in range(nkb):
                s_ = wblock4(src_ap[kb * 512:(kb + 1) * 512, :], cast=cast)
                for m in range(4):
                    for k in range(4):
                        kk = kb * 4 + k
                        first, last = (kk == 0), (kk == K - 1)
                        lhsT = wbf[s_][:, k * 512 + m * 128: k * 512 + (m + 1) * 128]
                        rhs = rhs_fn(kk)
                        sig = last or (m == 3 and k == 3)
                        if sig:
                            rd = [bf("wbfA%d" % s_), bf("wbfB%d" % s_)] + [bf(x) for x in rhs_names]
                        else:
                            rd = [bf("wbfA%d" % s_ if k < 2 else "wbfB%d" % s_)] + [bf(x) for x in rhs_names]
                        wr = [bank[m]] if (first or last) else []

                        def fn(e, lhsT=lhsT, rhs=rhs, first=first, last=last, m=m):
                            return e.matmul(psf[m][:, 0:ncols], lhsT, rhs, start=first, stop=last)
                        P.op("pe", fn, rd, wr, signal=sig)

        def proj_tiles(src2d, c0, ntile, K, rhs_fn, rhs_names, evac, ncols=NT, cast="dve"):
            for gg in range(ntile // 4):
                proj_group(src2d[:, c0 + gg * 512: c0 + (gg + 1) * 512], K, rhs_fn, rhs_names, ncols, cast)
                for j in range(4):
                    evac(gg * 4 + j, j)

        def colap(j):
            return cols[:, j:j + 1]

        def reduce_angle(ang, tmp, shape_r, names):
            for kbit in range(11, -1, -1):
                c = 2.0 * PI * (2 ** kbit)
                dve(A_ts(tmp, ang, c, c, ALU.is_ge, ALU.mult), r=names, w=names)
                dve(A_tt(ang, ang, tmp, ALU.subtract), r=names, w=names)

        def cossin(ang_src_fn, cos_out, sin_out, a1, a2, names):
            ang_src_fn(a1, 0.0)
            reduce_angle(a1, a2, None, names)
            dve(A_ts(a1, a1, -PI, None, ALU.add), r=names, w=names)
            act(A_act(sin_out, a1, AF.Sin, scale=1.0), r=names, w=names)
            dve(A_ts(sin_out, sin_out, -1.0, None, ALU.mult), r=names, w=names)
            ang_src_fn(a1, PI / 2)
            reduce_angle(a1, a2, None, names)
            dve(A_ts(a1, a1, -PI, None, ALU.add), r=names, w=names)
            act(A_act(cos_out, a1, AF.Sin, scale=1.0), r=names, w=names)
            dve(A_ts(cos_out, cos_out, -1.0, None, ALU.mult), r=names, w=names)

        ld("cst", consts[:, :], consts_d, w=["consts"])
        act(A_act(ident_bf[:, :], consts[:, 0:128], AF.Copy), r=["consts"], w=["ident"])
        dve(lambda e: e.memset(ones_bf[:, :], 1.0), w=["ones"])
        tril = consts[:, 128:256]
        iotaA = consts[:, 256:288]
        iotaB = consts[:, 288:304]

        def s5slot(i):
            return s5s[:, i * 32:(i + 1) * 32]

        def layer_body(L):
            if nlayers == 1:
                x_src, x_dst = xT, outT
            else:
                x_src = xT if L == 0 else xscr[(L - 1) % 2]
                x_dst = outT if L == nlayers - 1 else xscr[L % 2]

            ld("misc", cols[:, :], cols_d[L], w=["cols"])
            sgw_f = SF[:, 4096:5120]
            ld("misc", sgw_f, sgwT_d[L], w=["SFh1"])
            ld("misc", sgb[:, :], sgb_d[L], w=["sgb"])
            ld("misc", s5a[:, :], s5a_d[L], w=["s5a"])
            for nm_ in ["cols", "SFh1", "sgb", "s5a"]:
                bf(nm_).w = ("misc", P.dsem["misc"])
            for g in range(8):
                dve(A_tt(sgwT_bf[:, g * 128:(g + 1) * 128], sgw_f[:, g * 128:(g + 1) * 128], tril, ALU.mult),
                    r=["SFh1", "consts"], w=["sgwT"])
            for hh in range(2):
                ld("misc2", SF[:, 0:4096], bpad_d[L][:, hh * 4096:(hh + 1) * 4096], w=["SFh0"])
                act(A_act(bpad_bf[:, hh * 4096:(hh + 1) * 4096], SF[:, 0:4096], AF.Copy), r=["SFh0"], w=["bpad"])
            a_re, a_im, ldt = s5a[:, 0:32], s5a[:, 32:64], s5a[:, 64:96]
            AR, DT, MAG, TH, C1, S1, NR, DEN, CFR, CFI, T1, T2, R5R, R5I, T3 = [s5slot(i) for i in range(15)]
            SN = ["s5s"]
            dve(A_ts(AR, a_re, -1e-4, None, ALU.min), r=["s5a"], w=SN)
            act(A_act(DT, ldt, AF.Exp), r=["s5a"], w=SN)
            dve(A_tt(T1, AR, DT, ALU.mult), r=SN, w=SN)
            act(A_act(MAG, T1, AF.Exp), r=SN, w=SN)
            dve(A_tt(TH, a_im, DT, ALU.mult), r=SN + ["s5a"], w=SN)

            def ang_th(dst, off):
                dve(A_ts(dst, TH, 1.0, off, ALU.mult, ALU.add), r=SN, w=SN)
            cossin(ang_th, C1, S1, T1, T2, SN)
            dve(A_tt(NR, MAG, C1, ALU.mult), r=SN, w=SN)
            dve(A_tt(T3, MAG, S1, ALU.mult), r=SN, w=SN)
            dve(A_ts(NR, NR, -1.0, None, ALU.add), r=SN, w=SN)
            dve(A_tt(DEN, AR, AR, ALU.mult), r=SN, w=SN)
            dve(A_tt(T1, a_im, a_im, ALU.mult), r=SN + ["s5a"], w=SN)
            dve(A_tt(DEN, DEN, T1, ALU.add), r=SN, w=SN)
            dve(lambda e: e.reciprocal(out=DEN, in_=DEN), r=SN, w=SN)
            dve(A_tt(T1, NR, AR, ALU.mult), r=SN, w=SN)
            dve(A_tt(T2, T3, a_im, ALU.mult), r=SN + ["s5a"], w=SN)
            dve(A_tt(T1, T1, T2, ALU.add), r=SN, w=SN)
            dve(A_tt(CFR, T1, DEN, ALU.mult), r=SN, w=SN)
            dve(A_tt(T1, T3, AR, ALU.mult), r=SN, w=SN)
            dve(A_tt(T2, NR, a_im, ALU.mult), r=SN + ["s5a"], w=SN)
            dve(A_tt(T1, T1, T2, ALU.subtract), r=SN, w=SN)
            dve(A_tt(CFI, T1, DEN, ALU.mult), r=SN, w=SN)

            def ang_512(dst, off):
                dve(A_ts(dst, TH, float(NT), off, ALU.mult, ALU.add), r=SN, w=SN)
            cossin(ang_512, R5R, R5I, T1, T2, SN)
            TN = ["tabs", "SFh0"]
            tAc = tabA[:, 0:1024].rearrange("p (m a) -> p m a", a=32)
            tAs = tabA[:, 1024:2048].rearrange("p (m a) -> p m a", a=32)
            tBc = tabB[:, 0:512].rearrange("p (m b) -> p m b", b=16)
            tBs = tabB[:, 512:1024].rearrange("p (m b) -> p m b", b=16)
            sA1 = SF[:, 0:1024].rearrange("p (m a) -> p m a", a=32)
            sA2 = SF[:, 1024:2048].rearrange("p (m a) -> p m a", a=32)
            sB1 = SF[:, 2048:2560].rearrange("p (m b) -> p m b", b=16)
            sB2 = SF[:, 2560:3072].rearrange("p (m b) -> p m b", b=16)
            th_bA = TH.unsqueeze(2).to_broadcast([128, 32, 32])
            th_bB = TH.unsqueeze(2).to_broadcast([128, 32, 16])
            io_bA = iotaA.unsqueeze(1).to_broadcast([128, 32, 32])
            io_bB = iotaB.unsqueeze(1).to_broadcast([128, 32, 16])

            def ang_A(dst, off):
                dve(A_tt(dst, th_bA, io_bA, ALU.mult), r=SN + TN + ["consts"], w=TN)
                if off != 0.0:
                    dve(A_ts(dst, dst, off, None, ALU.add), r=TN, w=TN)

            def ang_B(dst, off):
                dve(A_tt(dst, th_bB, io_bB, ALU.mult), r=SN + TN + ["consts"], w=TN)
                if off != 0.0:
                    dve(A_ts(dst, dst, off, None, ALU.add), r=TN, w=TN)
            cossin(ang_A, tAc, tAs, sA1, sA2, TN)
            cossin(ang_B, tBc, tBs, sB1, sB2, TN)
            CN = ["SFh0"]
            for ch in range(4):
                cre = SF[:, 0:1024].rearrange("p (m c) -> p m c", c=128)
                cim = SF[:, 1024:2048].rearrange("p (m c) -> p m c", c=128)
                u1 = SF[:, 2048:3072].rearrange("p (m c) -> p m c", c=128)
                u2 = SF[:, 3072:4096].rearrange("p (m c) -> p m c", c=128)
                ld("misc2", SF[:, 0:1024], cpad_d[L][:, ch * 1024:(ch + 1) * 1024], w=CN)
                ld("misc2", SF[:, 1024:2048], cpad_d[L][:, 4096 + ch * 1024:4096 + (ch + 1) * 1024], w=CN)
                cfr_b = CFR[:, ch * 8:(ch + 1) * 8].unsqueeze(2).to_broadcast([128, 8, 128])
                cfi_b = CFI[:, ch * 8:(ch + 1) * 8].unsqueeze(2).to_broadcast([128, 8, 128])
                cpr = cp_bf[:, ch * 1024:(ch + 1) * 1024].rearrange("p (m c) -> p m c", c=128)
                cpi = cp_bf[:, 4096 + ch * 1024:4096 + (ch + 1) * 1024].rearrange("p (m c) -> p m c", c=128)
                dve(A_tt(u1, cre, cfr_b, ALU.mult), r=CN + SN, w=CN)
                dve(A_tt(u2, cim, cfi_b, ALU.mult), r=CN + SN, w=CN)
                dve(A_tt(cpr, u1, u2, ALU.subtract), r=CN, w=["cp"])
                dve(A_tt(u1, cre, cfi_b, ALU.mult), r=CN + SN, w=CN)
                dve(A_tt(u2, cim, cfr_b, ALU.mult), r=CN + SN, w=CN)
                dve(A_stt(cpi, u1, -1.0, u2, ALU.mult, ALU.subtract), r=CN, w=["cp"])
            dve(lambda e: e.memset(carry[:, :], 0.0), w=["carry"])

            memf = SF[:, 0:4096].rearrange("p (k m) -> p k m", m=NMEM)
            ld("misc2", memf, memT.rearrange("(k p) m -> p k m", p=128), w=["SFh0"])
            msq = SB[:, 0:4096].rearrange("p (k m) -> p k m", m=NMEM)
            act(A_act(msq, memf, AF.Square), r=["SFh0"], w=["SBq0"])
            items = [(ones_bf[:, :], msq[:, k, :], ["ones", "SBq0"]) for k in range(16)]
            mm_group(4, psf[4][:, 0:NMEM], items)
            dve(A_ts(tA[:, 0:NMEM], psf[4][:, 0:NMEM], 1.0 / D, EPS, ALU.mult, ALU.add), r=["bank4"], w=["tA"])
            act(A_act(tA[:, 0:NMEM], tA[:, 0:NMEM], AF.Sqrt), r=["tA"], w=["tA"])
            dve(lambda e: e.reciprocal(out=tA[:, 0:NMEM], in_=tA[:, 0:NMEM]), r=["tA"], w=["tA"])
            memn = SB[:, 4096:8192].rearrange("p (k m) -> p k m", m=NMEM)
            for k in range(16):
                dve(A_stt(memn[:, k, :], memf[:, k, :], colap(32 + k), tA[:, 0:NMEM], ALU.mult, ALU.mult),
                    r=["SFh0", "tA", "cols"], w=["SBq1"])
            def ev_k(mt, bk):
                act(A_act(kT_bf[:, mt * 256:(mt + 1) * 256], psf[bk][:, 0:NMEM], AF.Copy), r=["bank%d" % bk], w=["kT"])
            proj_tiles(wk[L], 0, 8, 16, lambda k: memn[:, k, :], ["SBq1"], ev_k, ncols=NMEM)
            for cg in range(2):
                for kb in range(4):
                    s_ = wblock4(wv[L][kb * 512:(kb + 1) * 512, cg * 512:(cg + 1) * 512], cast="dve")
                    for m2 in range(2):
                        for k in range(4):
                            kk = kb * 4 + k
                            first, last = (kk == 0), (kk == 15)
                            lhsT = memn[:, kk, m2 * 128:(m2 + 1) * 128]
                            rhs = wbf[s_][:, k * 512:(k + 1) * 512]
                            sig = last or (m2 == 1 and k == 3)

                            def fn(e, lhsT=lhsT, rhs=rhs, first=first, last=last, m2=m2):
                                return e.matmul(psf[m2][:, :], lhsT, rhs, start=first, stop=last)
                            P.op("pe", fn, [bf("wbfA%d" % s_), bf("wbfB%d" % s_), bf("SBq1")],
                                 [bank[m2]] if (first or last) else [], signal=sig)
                for m2 in range(2):
                    act(A_act(vtok_bf[:, m2 * 1024 + cg * 512: m2 * 1024 + (cg + 1) * 512], psf[m2][:, :], AF.Copy),
                        r=["bank%d" % m2], w=["vtok"])

            if DEBUG_PHASE == 1:
                raise _Stop()
            for ti in range(ntiles):
                t0 = ti * NT
                hT3 = hT[:, :].rearrange("p (k t) -> p k t", t=NT)
                xf = SF[:, :].rearrange("p (k t) -> p k t", t=NT)
                xsq = SB[:, 0:8192].rearrange("p (k t) -> p k t", t=NT)
                for k in range(16):
                    ld("xin", xf[:, k, :], x_src[k * 128:(k + 1) * 128, t0:t0 + NT], w=["SFh0", "SFh1"])
                for k in range(16):
                    act(A_act(xsq[:, k, :], xf[:, k, :], AF.Square), r=["SFh0", "SFh1"], w=["SBq0", "SBq1"])
                items = [(ones_bf[:, :], xsq[:, k, :], ["ones", "SBq0", "SBq1"]) for k in range(16)]
                mm_group(4, psf[4][:, :], items)
                dve(A_ts(rstd[:, :], psf[4][:, :], 1.0 / D, EPS, ALU.mult, ALU.add), r=["bank4"], w=["rstd"])
                act(A_act(rstd[:, :], rstd[:, :], AF.Sqrt), r=["rstd"], w=["rstd"])
                dve(lambda e: e.reciprocal(out=rstd[:, :], in_=rstd[:, :]), r=["rstd"], w=["rstd"])
                for k in range(16):
                    if k % 2 == 0:
                        dve(A_stt(hT3[:, k, :], xf[:, k, :], colap(k), rstd[:, :], ALU.mult, ALU.mult),
                            r=["SFh0", "SFh1", "rstd", "cols"], w=["hT%d" % k])
                    else:
                        pool(A_ts(tF[:, :], xf[:, k, :], colap(k), None, ALU.mult), r=["SFh0", "SFh1", "cols"], w=["tF"])
                        pool(A_tt(hT3[:, k, :], tF[:, :], rstd[:, :], ALU.mult), r=["tF", "rstd"], w=["hT%d" % k])

                def h_rhs(k):
                    return hT3[:, k, :]

                def win_tiles(c0, ntile, evac, cast="dve"):
                    proj_tiles(w_in[L], c0, ntile, 16, h_rhs, HT, evac, cast=cast)

                if DEBUG_PHASE == 2:
                    raise _Stop()
                vbf = SB[:, 0:4096].rearrange("p (g t) -> p g t", t=NT)
                vsq = SB[:, 4096:8192].rearrange("p (g t) -> p g t", t=NT)
                vtk = SB[:, 8192:12288].rearrange("p (c g d) -> p c g d", g=8, d=128)
                za = SF[:, 0:4096].rearrange("p (g t) -> p g t", t=NT)
                def ev_av(g, bk):
                    act(A_act(vbf[:, g, :], psf[bk][:, :], AF.Copy), r=["bank%d" % bk], w=["SBq0"])
                    act(A_act(vsq[:, g, :], psf[bk][:, :], AF.Square), r=["bank%d" % bk], w=["SBq1"])
                win_tiles(W, 8, ev_av)
                mm_group(4, psf[4][:, :], [(ones_bf[:, :], vbf[:, g, :], ["ones", "SBq0"]) for g in range(8)])
                mm_group(5, psf[5][:, :], [(ones_bf[:, :], vsq[:, g, :], ["ones", "SBq1"]) for g in range(8)])
                act(A_act(tA[:, :], psf[4][:, :], AF.Copy, scale=1.0 / W), r=["bank4"], w=["tA"])
                dve(A_tt(tB[:, :], tA[:, :], tA[:, :], ALU.mult), r=["tA"], w=["tB"])
                dve(A_stt(tB[:, :], psf[5][:, :], 1.0 / W, tB[:, :], ALU.mult, ALU.subtract), r=["bank5", "tB"], w=["tB"])
                dve(A_ts(tB[:, :], tB[:, :], EPS, None, ALU.add), r=["tB"], w=["tB"])
                act(A_act(tB[:, :], tB[:, :], AF.Sqrt), r=["tB"], w=["tB"])
                dve(lambda e: e.reciprocal(out=tB[:, :], in_=tB[:, :]), r=["tB"], w=["tB"])
                VN = ["vn%d" % g for g in range(8)]
                for g in range(8):
                    eng_, tt_, tn_ = (pool, tC, "tC") if g % 2 == 0 else (dve, tD, "tD")
                    eng_(A_tt(tt_[:, :], vbf[:, g, :], tA[:, :], ALU.subtract), r=["SBq0", "tA"], w=[tn_])
                    eng_(A_tt(tt_[:, :], tt_[:, :], tB[:, :], ALU.mult), r=[tn_, "tB"], w=[tn_])
                    eng_(A_ts(vbf[:, g, :], tt_[:, :], colap(48 + g), colap(56 + g), ALU.mult, ALU.add),
                         r=[tn_, "cols"], w=[VN[g]])
                for g in range(8):
                    for c in range(4):
                        P.op("pe", (lambda e, g=g, c=c: e.transpose(pst[:, c * 128:(c + 1) * 128],
                                                                     vbf[:, g, c * 128:(c + 1) * 128], ident_bf[:, :])),
                             [bf(VN[g]), bf("ident")], [bank[7]], signal=(c == 3))
                    pstv = pst[:, 0:512].rearrange("p (c d) -> p c d", d=128)
                    act(A_act(vtk[:, :, g, :], pstv, AF.Copy), r=["bank7"], w=["SBq2"])
                alias(["SBq0"], VN)
                for g in range(8):
                    for c in range(4):
                        P.op("pe", (lambda e, g=g, c=c: e.matmul(psf[6][:, c * 128:(c + 1) * 128], vtk[:, c, g, :],
                                                                  sgwT_bf[:, g * 128:(g + 1) * 128], start=True, stop=True)),
                             [bf("SBq2"), bf("sgwT")], [bank[6]], signal=(c == 3))
                    zv = psf[6][:, :].rearrange("p (c t) -> p c t", t=128)
                    bb = sgb[:, g * 128:(g + 1) * 128].unsqueeze(1).to_broadcast([128, 4, 128])
                    dve(A_tt(za[:, g, :].rearrange("p (c t) -> p c t", t=128), zv, bb, ALU.add),
                        r=["bank6", "sgb"], w=["SFh0"])
                ya3 = ya[:, :].rearrange("p (g t) -> p g t", t=NT)
                def ev_au(g, bk):
                    dve(A_tt(za[:, g, :], psf[bk][:, :], za[:, g, :], ALU.mult), r=["bank%d" % bk, "SFh0"], w=["SFh0"])
                win_tiles(0, 8, ev_au, cast="act")

                def ev_ag(g, bk):
                    tt_, tn_ = (tE, "tE") if g % 2 == 0 else (tF, "tF")
                    act(A_act(tt_[:, :], psf[bk][:, :], AF.Silu), r=["bank%d" % bk], w=[tn_])
                    pool(A_tt(ya3[:, g, :], za[:, g, :], tt_[:, :], ALU.mult), r=["SFh0", tn_], w=["ya"])
                win_tiles(2 * W, 8, ev_ag)

                if DEBUG_PHASE == 3:
                    raise _Stop()
                qT = SB[:, 0:4096].rearrange("p (g t) -> p g t", t=NT)
                ex = SB[:, 4096:5120].rearrange("p (m t) -> p m t", t=NT)
                ox = SF[:, 4096:8192].rearrange("p (g t) -> p g t", t=NT)
                def ev_q(g, bk):
                    act(A_act(qT[:, g, :], psf[bk][:, :], AF.Copy), r=["bank%d" % bk], w=["SBq0"])
                win_tiles(5 * W, 8, ev_q)
                kT3 = kT_bf[:, :].rearrange("p (g m) -> p g m", m=NMEM)
                vt3 = vtok_bf[:, :].rearrange("p (m c) -> p m c", c=1024)
                for hd in range(4):
                    for m2 in range(2):
                        items = [(kT3[:, 2 * hd + dd, m2 * 128:(m2 + 1) * 128], qT[:, 2 * hd + dd, :], ["kT", "SBq0"])
                                 for dd in range(2)]
                        mm_group(4 + m2, psf[4 + m2][:, :], items)
                        act(A_act(ex[:, m2, :], psf[4 + m2][:, :], AF.Exp, scale=1.0 / 16.0),
                            r=["bank%d" % (4 + m2)], w=["SBq1"])
                    mm_group(6, psf[6][:, :], [(ones_bf[:, :], ex[:, m2, :], ["ones", "SBq1"]) for m2 in range(2)])
                    dve(lambda e: e.reciprocal(out=tA[:, :], in_=psf[6][:, :]), r=["bank6"], w=["tA"])
                    for dd in range(2):
                        ch = 2 * hd + dd
                        items = [(vt3[:, m2, ch * 128:(ch + 1) * 128], ex[:, m2, :], ["vtok", "SBq1"])
                                 for m2 in range(2)]
                        mm_group(6, psf[6][:, :], items)
                        dve(A_tt(ox[:, ch, :], psf[6][:, :], tA[:, :], ALU.mult), r=["bank6", "tA"], w=["SFh1"])
                yx3 = yx[:, :].rearrange("p (g t) -> p g t", t=NT)
                def ev_xg(g, bk):
                    tt_, tn_ = (tE, "tE") if g % 2 == 0 else (tF, "tF")
                    act(A_act(tt_[:, :], psf[bk][:, :], AF.Silu), r=["bank%d" % bk], w=[tn_])
                    pool(A_tt(yx3[:, g, :], ox[:, g, :], tt_[:, :], ALU.mult), r=["SFh1", tn_], w=["yx"])
                win_tiles(6 * W, 8, ev_xg)

                if DEBUG_PHASE == 4:
                    raise _Stop()
                uT = SB[:, 0:4096].rearrange("p (g t) -> p g t", t=NT)
                ypre = SF[:, 0:4096].rearrange("p (g t) -> p g t", t=NT)
                hre = SB[:, 4096:4608]
                him = SB[:, 4608:5120]
                bp4 = bpad_bf[:, :].rearrange("p (r m c) -> p r m c", r=2, c=128)
                cp4 = cp_bf[:, :].rearrange("p (r m c) -> p r m c", r=2, c=128)
                tAc4 = tabA[:, :].rearrange("p (r m a) -> p r m a", r=2, a=32)
                tBc4 = tabB[:, :].rearrange("p (r m b) -> p r m b", r=2, b=16)
                alias(S5T, ["SFh1"])
                alias(["hre", "him"], ["SBq1"])
                def ev_bx(g, bk):
                    act(A_act(uT[:, g, :], psf[bk][:, :], AF.Copy), r=["bank%d" % bk], w=["SBq0"])
                win_tiles(3 * W, 8, ev_bx)

                def v3(tt_):
                    return tt_[:, :].rearrange("p (a b) -> p a b", b=16)
                gre, gim = SF[:, 4096:4608], SF[:, 4608:5120]
                Gre, Gim = SF[:, 5120:5632], SF[:, 5632:6144]
                m1, m2_ = SF[:, 6144:6656], SF[:, 6656:7168]
                p1, p2 = SF[:, 7168:7680], SF[:, 7680:8192]
                TABS = [(tA, tB, "tA", "tB"), (tC, tD, "tC", "tD")]

                def emit_tables(m):
                    cT_, sT_, cn_, sn_ = TABS[m % 2]
                    cA = tAc4[:, 0, m, :].unsqueeze(2).to_broadcast([128, 32, 16])
                    sA = tAc4[:, 1, m, :].unsqueeze(2).to_broadcast([128, 32, 16])
                    cB = tBc4[:, 0, m, :].unsqueeze(1).to_broadcast([128, 32, 16])
                    sB = tBc4[:, 1, m, :].unsqueeze(1).to_broadcast([128, 32, 16])
                    pool(A_tt(v3(tE), cA, cB, ALU.mult), r=["tabs"], w=["tE"])
                    pool(A_tt(v3(tF), sA, sB, ALU.mult), r=["tabs"], w=["tF"])
                    pool(A_tt(cT_[:, :], tE[:, :], tF[:, :], ALU.subtract), r=["tE", "tF"], w=[cn_])
                    pool(A_tt(v3(tE), sA, cB, ALU.mult), r=["tabs"], w=["tE"])
                    pool(A_tt(v3(tF), cA, sB, ALU.mult), r=["tabs"], w=["tF"])
                    dve(A_tt(sT_[:, :], tE[:, :], tF[:, :], ALU.add), r=["tE", "tF"], w=[sn_])

                emit_tables(0)
                for q in range(8):
                    for mm_ in range(4):
                        m = 4 * q + mm_
                        cosT, sinT, cn_, sn_ = TABS[m % 2]
                        mm_group(4, psf[4][:, :], [(bp4[:, 0, m, :], uT[:, q, :], ["bpad", "SBq0"])])
                        mm_group(5, psf[5][:, :], [(bp4[:, 1, m, :], uT[:, q, :], ["bpad", "SBq0"])])
                        dve(A_tt(m1, psf[4][:, :], cosT[:, :], ALU.mult), r=["bank4", cn_], w=["m1"])
                        dve(A_tt(m2_, psf[5][:, :], sinT[:, :], ALU.mult), r=["bank5", sn_], w=["m2"])
                        dve(A_tt(gre, m1, m2_, ALU.add), r=["m1", "m2"], w=["gre"])
                        dve(A_tt(m1, psf[5][:, :], cosT[:, :], ALU.mult), r=["bank5", cn_], w=["m1"])
                        dve(A_tt(m2_, psf[4][:, :], sinT[:, :], ALU.mult), r=["bank4", sn_], w=["m2"])
                        dve(A_tt(gim, m1, m2_, ALU.subtract), r=["m1", "m2"], w=["gim"])
                        rb = s5slot(2)[:, m:m + 1].to_broadcast([128, NT])
                        cre_i = carry[:, m:m + 1]
                        cim_i = carry[:, 32 + m:33 + m]
                        dve((lambda e, rb=rb, cre_i=cre_i: e.tensor_tensor_scan(out=Gre, data0=rb, data1=gre, initial=cre_i,
                                                                                op0=ALU.mult, op1=ALU.add)),
                            r=["gre", "s5s", "carry"], w=["Gre"])
                        dve((lambda e, rb=rb, cim_i=cim_i: e.tensor_tensor_scan(out=Gim, data0=rb, data1=gim, initial=cim_i,
                                                                                op0=ALU.mult, op1=ALU.add)),
                            r=["gim", "s5s", "carry"], w=["Gim"])
                        r5r, r5i = s5slot(12)[:, m:m + 1], s5slot(13)[:, m:m + 1]
                        gl_r, gl_i = Gre[:, NT - 1:NT], Gim[:, NT - 1:NT]
                        tmpc = s5slot(15)[:, m:m + 1]
                        dve(A_tt(tmpc, gl_i, r5i, ALU.mult), r=["Gim", "s5s"], w=["tmpc"])
                        dve(A_stt(cre_i, gl_r, r5r, tmpc, ALU.mult, ALU.subtract), r=["Gre", "tmpc", "s5s"], w=["carry"])
                        dve(A_tt(tmpc, gl_i, r5r, ALU.mult), r=["Gim", "s5s"], w=["tmpc"])
                        dve(A_stt(cim_i, gl_r, r5i, tmpc, ALU.mult, ALU.add), r=["Gre", "tmpc", "s5s"], w=["carry"])
                        if m + 1 < 32:
                            emit_tables(m + 1)
                        pool(A_tt(p1, Gre, cosT[:, :], ALU.mult), r=["Gre", cn_], w=["p1"])
                        pool(A_tt(p2, Gim, sinT[:, :], ALU.mult), r=["Gim", sn_], w=["p2"])
                        pool(A_tt(hre, p1, p2, ALU.subtract), r=["p1", "p2"], w=["hre"])
                        dve(A_tt(m1, Gre, sinT[:, :], ALU.mult), r=["Gre", sn_], w=["m1"])
                        dve(A_tt(m2_, Gim, cosT[:, :], ALU.mult), r=["Gim", cn_], w=["m2"])
                        dve(A_tt(him, m1, m2_, ALU.add), r=["m1", "m2"], w=["him"])
                        st = (mm_ == 0)
                        last = (mm_ == 3)
                        P.op("pe", (lambda e, m=m, st=st: e.matmul(psf[6][:, :], cp4[:, 0, m, :], hre, start=st, stop=False)),
                             [bf("cp"), bf("hre")], [bank[6]] if st else [], signal=True)
                        P.op("pe", (lambda e, m=m, last=last: e.matmul(psf[6][:, :], cp4[:, 1, m, :], him, start=False, stop=last)),
                             [bf("cp"), bf("him")], [bank[6]] if last else [], signal=True)
                    dve(A_stt(ypre[:, q, :], uT[:, q, :], colap(72 + q), psf[6][:, :], ALU.mult, ALU.add),
                        r=["bank6", "SBq0", "cols"], w=["SFh0"])
                alias(["SFh1"], S5T)
                alias(["SBq1"], ["hre", "him"])
                ygl = SB[:, 8192:12288].rearrange("p (g t) -> p g t", t=NT)
                YP = ["yp%d" % q for q in range(8)]
                for q in range(8):
                    eng_, tt_, tn_ = (pool, tC, "tC") if q % 2 == 0 else (dve, tD, "tD")
                    eng_(A_tt(tt_[:, :], ypre[:, q, :], ypre[:, q, :], ALU.mult), r=["SFh0"], w=[tn_])
                    eng_(A_ts(tt_[:, :], tt_[:, :], 0.044715, 1.0, ALU.mult, ALU.add), r=[tn_], w=[tn_])
                    eng_(A_tt(tt_[:, :], tt_[:, :], ypre[:, q, :], ALU.mult), r=[tn_, "SFh0"], w=[tn_])
                    act(A_act(tt_[:, :], tt_[:, :], AF.Sigmoid, scale=1.5957691216057308), r=[tn_], w=[tn_])
                    eng_(A_tt(ypre[:, q, :], ypre[:, q, :], tt_[:, :], ALU.mult), r=[tn_, "SFh0"], w=[YP[q]])
                    act(A_act(ygl[:, q, :], ypre[:, q, :], AF.Copy), r=[YP[q]], w=["SBq2"])
                yb3 = yb[:, :].rearrange("p (g t) -> p g t", t=NT)
                def ev_glu(g, bk):
                    tt_, tn_ = (tE, "tE") if g % 2 == 0 else (tF, "tF")
                    act(A_act(tt_[:, :], psf[bk][:, :], AF.Sigmoid, bias=colap(64 + g)), r=["bank%d" % bk, "cols"], w=[tn_])
                    pool(A_tt(yb3[:, g, :], ypre[:, g, :], tt_[:, :], ALU.mult), r=[YP[g], tn_], w=["yb"])
                proj_tiles(glu_w[L], 0, 8, 8, lambda k: ygl[:, k, :], ["SBq2"], ev_glu)
                alias(["SFh0"], YP)

                def ev_bg(g, bk):
                    tt_, tn_ = (tE, "tE") if g % 2 == 0 else (tF, "tF")
                    act(A_act(tt_[:, :], psf[bk][:, :], AF.Silu), r=["bank%d" % bk], w=[tn_])
                    pool(A_tt(yb3[:, g, :], yb3[:, g, :], tt_[:, :], ALU.mult), r=["yb", tn_], w=["yb"])
                win_tiles(4 * W, 8, ev_bg)

                if DEBUG_PHASE == 5:
                    raise _Stop()
                mg = SB[:, 8192:16384].rearrange("p (k t) -> p k t", t=NT)
                MG = ["SBq2", "SBq3"]
                ys = [ya3, yb3, yx3]
                ynm = ["ya", "yb", "yx"]
                GT = ["gt%d" % j for j in range(4)]
                AC = ["acc%d" % j for j in range(4)]
                alias(GT + AC + ["mtmp0", "mtmp1"], ["SFh0", "SFh1"])
                gts = [SF[:, j * NT:(j + 1) * NT] for j in range(4)]
                accs = [SF[:, (4 + j) * NT:(5 + j) * NT] for j in range(4)]
                mtmp = [SF[:, (8 + j) * NT:(9 + j) * NT] for j in range(2)]
                for mq in range(4):
                    for n in range(3):
                        def ev_gate(g, bk):
                            act(A_act(gts[bk], psf[bk][:, :], AF.Sigmoid), r=["bank%d" % bk], w=[GT[bk]])
                        proj_tiles(w_in[L], 7 * W + n * D + mq * 512, 4, 16, h_rhs, HT, ev_gate)

                        def ev_br(g, bk, n=n, mq=mq):
                            if n == 0:
                                dve(A_tt(accs[bk], psf[bk][:, :], gts[bk], ALU.mult), r=["bank%d" % bk, GT[bk]], w=[AC[bk]])
                            else:
                                tm = mtmp[bk % 2]
                                tmn = "mtmp%d" % (bk % 2)
                                dve(A_tt(tm, psf[bk][:, :], gts[bk], ALU.mult), r=["bank%d" % bk, GT[bk]], w=[tmn])
                                if n == 1:
                                    dve(A_tt(accs[bk], accs[bk], tm, ALU.add), r=[AC[bk], tmn], w=[AC[bk]])
                                else:
                                    dve(A_tt(mg[:, mq * 4 + bk, :], accs[bk], tm, ALU.add), r=[AC[bk], tmn], w=MG)
                        proj_tiles(w_br[L][n], mq * 512, 4, 8, (lambda k, n=n: ys[n][:, k, :]), [ynm[n]], ev_br,
                                   cast="act")
                alias(["SFh0", "SFh1"], GT + AC + ["mtmp0", "mtmp1"])

                if DEBUG_PHASE in (6, 10):
                    raise _Stop()
                of = SF[:, :].rearrange("p (k t) -> p k t", t=NT)
                osq = SB[:, 0:8192].rearrange("p (k t) -> p k t", t=NT)
                SFN = ["SFh0", "SFh1"]
                def ev_out(mt, bk):
                    dve(A_cp(of[:, mt, :], psf[bk][:, :]), r=["bank%d" % bk], w=SFN)
                    act(A_act(osq[:, mt, :], of[:, mt, :], AF.Square), r=SFN, w=["SBq0", "SBq1"])
                XB = [(tC, "tC"), (tD, "tD"), (tE, "tE"), (tF, "tF")]

                def ld_x(mt):
                    xb_, xn = XB[mt % 4]
                    ld("xin2_%d" % (mt % 4), xb_[:, :], x_src[mt * 128:(mt + 1) * 128, t0:t0 + NT], w=[xn])
                for mt in range(4):
                    ld_x(mt)
                proj_tiles(w_out[L], 0, 16, 16, lambda k: mg[:, k, :], MG, ev_out, cast="act")
                if DEBUG_PHASE in (7, 71, 72, 73):
                    raise _Stop()
                mm_group(4, psf[4][:, :], [(ones_bf[:, :], osq[:, k, :], ["ones", "SBq0", "SBq1"]) for k in range(16)])
                dve(A_ts(rstd[:, :], psf[4][:, :], 1.0 / D, EPS, ALU.mult, ALU.add), r=["bank4"], w=["rstd"])
                act(A_act(rstd[:, :], rstd[:, :], AF.Sqrt), r=["rstd"], w=["rstd"])
                dve(lambda e: e.reciprocal(out=rstd[:, :], in_=rstd[:, :]), r=["rstd"], w=["rstd"])
                if DEBUG_PHASE == 8:
                    raise _Stop()
                for mt in range(16):
                    xb_, xn = XB[mt % 4]
                    dve(A_stt(of[:, mt, :], of[:, mt, :], colap(16 + mt), rstd[:, :], ALU.mult, ALU.mult),
                        r=SFN + ["rstd", "cols"], w=SFN)
                    dve(A_tt(xb_[:, :], xb_[:, :], of[:, mt, :], ALU.add), r=[xn] + SFN, w=[xn])
                    if DEBUG_PHASE == 9:
                        continue
                    P.dma("xout_%d" % (mt % 4),
                          (lambda e, xb_=xb_, mt=mt, t0=t0, x_dst=x_dst: e.dma_start(
                              out=x_dst[mt * 128:(mt + 1) * 128, t0:t0 + NT], in_=xb_[:, :])),
                          [bf(xn)], [])
                    if mt + 4 < 16:
                        ld_x(mt + 4)
            lw = [(s_, v_) for s_, v_ in P.dsem.items() if s_.startswith("xout_")]
            P.ops["sp"].append((lw, None, None, 0))

        try:
            for L_ in range(nlayers):
                layer_body(L_)
        except _Stop:
            if DEBUG_PHASE == 72:
                dbg = nc.dram_tensor("dbg", [128, 56 * NT], F32, kind="ExternalOutput").ap()
                srcs = ([(hT[:, k * NT:(k + 1) * NT], "hT%d" % k) for k in range(16)]
                        + [(ya[:, k * NT:(k + 1) * NT], "ya") for k in range(8)]
                        + [(yb[:, k * NT:(k + 1) * NT], "yb") for k in range(8)]
                        + [(yx[:, k * NT:(k + 1) * NT], "yx") for k in range(8)]
                        + [(SB[:, 8192 + k * NT: 8192 + (k + 1) * NT], "SBq2" if k < 8 else "SBq3") for k in range(16)])
                for i, (src, nm) in enumerate(srcs):
                    tt_, tn_ = [(tC, "tC"), (tD, "tD"), (tE, "tE"), (tF, "tF")][i % 4]
                    dve(A_cp(tt_[:, :], src), r=[nm], w=[tn_])
                    P.dma("xout_%d" % (i % 2), (lambda e, tt_=tt_, i=i: e.dma_start(out=dbg[:, i * NT:(i + 1) * NT], in_=tt_[:, :])),
                          [bf(tn_)], [])
                for k in range(16):
                    P.dma("xout_%d" % (k % 2), (lambda e, k=k: e.dma_start(out=outT[k * 128:(k + 1) * 128, 0:NT],
                                                                         in_=SF[:, k * NT:(k + 1) * NT])),
                          [bf("SFh0"), bf("SFh1")], [])

        final_waits = [(s, v) for s, v in P.dsem.items() if s.startswith("xout_")]
        P.ops["sp"].append((final_waits, None, None, 0))

        semnames = set(["pe", "act", "dve", "pool"]) | set(P.dsem.keys())
        sems = {n: sem("s_" + n) for n in sorted(semnames)}
        with nc.Block() as block:
            def replay(engname):
                def f(e):
                    for waits, fn, sname, inc in P.ops[engname]:
                        for (s, v) in waits:
                            e.wait_ge(sems[s], v)
                        if fn is None:
                            continue
                        inst = fn(e)
                        if sname is not None:
                            inst.then_inc(sems[sname], inc)
                return f
            block.tensor(replay("pe"))
            block.scalar(replay("act"))
            block.vector(replay("dve"))
            block.gpsimd(replay("pool"))
            block.sync(replay("sp"))
    return nc


def _consts():
    c = np.zeros((128, 304), np.float32)
    c[:, 0:128] = np.eye(128, dtype=np.float32)
    s = np.arange(128)[:, None]
    t = np.arange(128)[None, :]
    c[:, 128:256] = (s <= t).astype(np.float32)
    c[:, 256:288] = (16.0 * np.arange(32, dtype=np.float32))[None, :]
    c[:, 288:304] = np.arange(16, dtype=np.float32)[None, :]
    return c


def _layer_layouts(inp, layers):
    f = np.float32
    out = {}

    def colmaj(v, n):
        return np.ascontiguousarray(v.reshape(n, 128).T)

    cols, sgwT, sgb, s5a, bpad, cpad = [], [], [], [], [], []
    for l in layers:
        c = np.concatenate([
            colmaj(inp["pre_norm_g"][l], 16), colmaj(inp["post_norm_g"][l], 16), colmaj(inp["mem_norm_g"][l], 16),
            colmaj(inp["sg_ln_g"][l], 8), colmaj(inp["sg_ln_b"][l], 8), colmaj(inp["glu_b"][l], 8),
            colmaj(inp["ssm_d"][l].reshape(-1), 8)], axis=1).astype(f)
        cols.append(c)
        sgwT.append(np.ascontiguousarray(inp["sg_w"][l].transpose(2, 0, 1)).reshape(128, 8 * 128).astype(f))
        sgb.append(np.ascontiguousarray(np.broadcast_to(inp["sg_b"][l].reshape(1, 8 * 128), (128, 8 * 128))).astype(f))

        def pairlay(a):
            return a.reshape(32, 2, 64).transpose(1, 2, 0).reshape(128, 32)
        ldt = np.broadcast_to(inp["ssm_log_dt"][l][:, None], (64, 64))
        s5a.append(np.concatenate([pairlay(inp["ssm_a_re"][l]), pairlay(inp["ssm_a_im"][l]), pairlay(ldt)], axis=1).astype(f))
        bp = np.zeros((2, 32, 128, 128), f)
        cp = np.zeros((2, 32, 128, 128), f)
        for ri, (bsrc, csrc) in enumerate([(inp["ssm_b_re"][l], inp["ssm_c_re"][l]), (inp["ssm_b_im"][l], inp["ssm_c_im"][l])]):
            for m in range(32):
                for s in range(2):
                    g = 2 * m + s
                    gl = g % 8
                    bp[ri, m, gl * 16:(gl + 1) * 16, s * 64:(s + 1) * 64] = bsrc[g].T
                    cp[ri, m, s * 64:(s + 1) * 64, gl * 16:(gl + 1) * 16] = csrc[g].T
        bpad.append(np.ascontiguousarray(bp.transpose(2, 0, 1, 3)).reshape(128, 2 * 32 * 128))
        cpad.append(np.ascontiguousarray(cp.transpose(2, 0, 1, 3)).reshape(128, 2 * 32 * 128))
    out["cols"] = np.stack(cols)
    out["sgwT"] = np.stack(sgwT)
    out["sgb"] = np.stack(sgb)
    out["s5a"] = np.stack(s5a)
    out["bpad"] = np.stack(bpad)
    out["cpad"] = np.stack(cpad)
    for k_src, k_dst in [("w_in", "w_in"), ("glu_w", "glu_w"), ("xa_wk", "wk"), ("xa_wv", "wv"),
                         ("w_branch", "w_br"), ("w_out", "w_out")]:
        out[k_dst] = np.ascontiguousarray(inp[k_src][list(layers)]).astype(f)
    out["consts"] = _consts()
    return out


_PROG_CACHE = {}


def _get_prog(ntiles, nlayers):
    key = (ntiles, nlayers)
    if key not in _PROG_CACHE:
        _PROG_CACHE[key] = build_program(ntiles, nlayers)
    return _PROG_CACHE[key]


FUSED = True
NCORES = 4


def run_layers(x, mem, inp, layers, ntiles):
    nb = x.shape[0]
    lay = _layer_layouts(inp, layers)
    nc = _get_prog(ntiles, len(layers))
    in_maps = []
    for b in range(nb):
        m = dict(lay)
        m["xT"] = np.ascontiguousarray(x[b].T)
        m["memT"] = np.ascontiguousarray(mem[b].T)
        in_maps.append(m)
    res = run_bass_kernel_spmd(nc, in_maps, core_ids=list(range(nb)))
    return np.stack([np.ascontiguousarray(res.results[b]["outT"].T) for b in range(nb)])


def kernel(**inputs):
    inp = {k: np.asarray(v) for k, v in inputs.items()}
    x = inp["x"].astype(np.float32)
    mem = inp["mem"].astype(np.float32)
    ntiles = x.shape[1] // NT
    if FUSED:
        return run_layers(x, mem, inp, list(range(DEPTH)), ntiles).astype(np.float32)
    for l in range(DEPTH):
        x = run_layers(x, mem, inp, [l], ntiles)
    return x.astype(np.float32)
```

```python
import math
from contextlib import ExitStack
import numpy as np
import concourse.bass as bass
import concourse.mybir as mybir
from concourse.bass_utils import run_bass_kernel_spmd

F32 = mybir.dt.float32
BF16 = mybir.dt.bfloat16
AF = mybir.ActivationFunctionType
ALU = mybir.AluOpType
PI = math.pi

D = 2048
W = 1024
NIN = 13312
NMEM = 256
NT = 512
DEPTH = 4
EPS = 1e-6
ENGS = ("pe", "act", "dve", "pool", "sp")


DEBUG_PHASE = 0


class _Stop(Exception):
    pass


class Buf:
    __slots__ = ("w", "r", "const")

    def __init__(self, const=False):
        self.w = None
        self.r = {}
        self.const = const


class Prog:
    def __init__(self):
        self.ops = {e: [] for e in ENGS}
        self.cnt = {e: 0 for e in ENGS}
        self.waited = {e: {} for e in ENGS}
        self.dsem = {}

    def _deps(self, eng, reads, writes):
        deps = []
        for b in reads:
            if b.w is not None:
                deps.append(b.w)
        for b in writes:
            if b.w is not None:
                deps.append(b.w)
            deps.extend(b.r.items())
        out = []
        wt = self.waited[eng]
        for (s, v) in deps:
            if wt.get(s, 0) >= v:
                continue
            wt[s] = v
            out.append((s, v))
        return out

    def _reg(self, tk, reads, writes):
        for b in reads:
            if not b.const:
                if b.r.get(tk[0], 0) < tk[1]:
                    b.r[tk[0]] = tk[1]
        for b in writes:
            b.w = tk
            b.r = {}

    def op(self, eng, fn, reads=(), writes=(), signal=True):
        waits = self._deps(eng, reads, writes)
        tk = None
        if signal:
            self.cnt[eng] += 1
            tk = (eng, self.cnt[eng])
            self._reg(tk, reads, writes)
        self.ops[eng].append((waits, fn, eng if signal else None, 1))
        return tk

    def dma(self, semname, fn, reads=(), writes=(), queue="sp"):
        waits = self._deps(queue, reads, writes)
        self.dsem[semname] = self.dsem.get(semname, 0) + 16
        tk = (semname, self.dsem[semname])
        self._reg(tk, reads, writes)
        self.ops[queue].append((waits, fn, semname, 16))
        return tk


def build_program(ntiles, nlayers):
    ntok = ntiles * NT
    nc = bass.Bass("TRN2", target_bir_lowering=False)
    P = Prog()

    def din(name, shape):
        return nc.dram_tensor(name, list(shape), F32, kind="ExternalInput").ap()

    xT = din("xT", [D, ntok])
    memT = din("memT", [D, NMEM])
    outT = nc.dram_tensor("outT", [D, ntok], F32, kind="ExternalOutput").ap()
    if nlayers > 1:
        xscr = [nc.dram_tensor("xscr%d" % i, [D, ntok], F32, kind="Internal").ap() for i in range(2)]
    w_in = din("w_in", [nlayers, D, NIN])
    glu_w = din("glu_w", [nlayers, W, W])
    wk = din("wk", [nlayers, D, W])
    wv = din("wv", [nlayers, D, W])
    w_br = din("w_br", [nlayers, 3, W, D])
    w_out = din("w_out", [nlayers, D, D])
    cols_d = din("cols", [nlayers, 128, 80])
    sgwT_d = din("sgwT", [nlayers, 128, 8 * 128])
    sgb_d = din("sgb", [nlayers, 128, 8 * 128])
    s5a_d = din("s5a", [nlayers, 128, 3 * 32])
    bpad_d = din("bpad", [nlayers, 128, 2 * 32 * 128])
    cpad_d = din("cpad", [nlayers, 128, 2 * 32 * 128])
    consts_d = din("consts", [128, 128 + 128 + 32 + 16])

    es = ExitStack()

    def sb(name, shape, dt=F32):
        return es.enter_context(nc.sbuf_tensor("sb_" + name, list(shape), dt))

    def sem(name):
        return es.enter_context(nc.semaphore(name))

    with es:
        consts = sb("consts", [128, 304])
        ident_bf = sb("ident_bf", [128, 128], BF16)
        ones_bf = sb("ones_bf", [128, 128], BF16)
        cols = sb("cols", [128, 80])
        sgwT_bf = sb("sgwT_bf", [128, 8 * 128], BF16)
        sgb = sb("sgb", [128, 8 * 128])
        bpad_bf = sb("bpad_bf", [128, 2 * 32 * 128], BF16)
        cp_bf = sb("cp_bf", [128, 2 * 32 * 128], BF16)
        s5a = sb("s5a", [128, 96])
        s5s = sb("s5s", [128, 16 * 32])
        tabA = sb("tabA", [128, 2 * 32 * 32])
        tabB = sb("tabB", [128, 2 * 32 * 16])
        carry = sb("carry", [128, 2 * 32])
        kT_bf = sb("kT_bf", [128, 8 * 256], BF16)
        vtok_bf = sb("vtok_bf", [128, 2 * 1024], BF16)
        NSLOT = 2
        stage = [sb("stage%d" % i, [128, 16 * 128]) for i in range(NSLOT)]
        wbf = [sb("wbf%d" % i, [128, 16 * 128], BF16) for i in range(NSLOT)]
        hT = sb("hT", [128, 16 * NT], BF16)
        ya = sb("ya", [128, 8 * NT], BF16)
        yb = sb("yb", [128, 8 * NT], BF16)
        yx = sb("yx", [128, 8 * NT], BF16)
        SF = sb("SF", [128, 8192])
        SB = sb("SB", [128, 16384], BF16)
        rstd = sb("rstd", [128, NT])
        tA = sb("tA", [128, NT])
        tB = sb("tB", [128, NT])
        tC = sb("tC", [128, NT])
        tD = sb("tD", [128, NT])
        tE = sb("tE", [128, NT])
        tF = sb("tF", [128, NT])
        psf = [es.enter_context(nc.psum_tensor("ps%d" % i, [128, NT], F32)) for i in range(7)]
        pst = es.enter_context(nc.psum_tensor("pst", [128, 1024], BF16))

        B = {}

        def bf(name, const=False):
            if name not in B:
                B[name] = Buf(const)
            return B[name]

        bank = [bf("bank%d" % i) for i in range(8)]

        def alias(dst, src):
            for d in dst:
                b = bf(d)
                for s_ in src:
                    o = bf(s_)
                    for k_, v_ in o.r.items():
                        b.r[k_] = max(b.r.get(k_, 0), v_)
                    if o.w is not None:
                        b.r[o.w[0]] = max(b.r.get(o.w[0], 0), o.w[1])

        HT = ["hT%d" % k for k in range(16)]
        S5T = ["gre", "gim", "Gre", "Gim", "m1", "m2", "p1", "p2"]

        def dve(fn, r=(), w=()):
            return P.op("dve", fn, [bf(x) for x in r], [bf(x) for x in w])

        def pool(fn, r=(), w=()):
            return P.op("pool", fn, [bf(x) for x in r], [bf(x) for x in w])

        def act(fn, r=(), w=()):
            return P.op("act", fn, [bf(x) for x in r], [bf(x) for x in w])

        def ld(semname, out_ap, in_ap, w=(), r=()):
            return P.dma(semname, lambda e: e.dma_start(out=out_ap, in_=in_ap),
                         [bf(x) for x in r], [bf(x) for x in w])

        def A_act(out, in_, func, bias=None, scale=1.0):
            if bias is None:
                return lambda e: e.activation(out=out, in_=in_, func=func, scale=scale)
            return lambda e: e.activation(out=out, in_=in_, func=func, bias=bias, scale=scale)

        def A_tt(out, in0, in1, op):
            return lambda e: e.tensor_tensor(out=out, in0=in0, in1=in1, op=op)

        def A_ts(out, in0, s1, s2, op0, op1=None):
            if op1 is None:
                return lambda e: e.tensor_scalar(out=out, in0=in0, scalar1=s1, scalar2=None, op0=op0)
            return lambda e: e.tensor_scalar(out=out, in0=in0, scalar1=s1, scalar2=s2, op0=op0, op1=op1)

        def A_stt(out, in0, scalar, in1, op0, op1):
            return lambda e: e.scalar_tensor_tensor(out=out, in0=in0, scalar=scalar, in1=in1, op0=op0, op1=op1)

        def A_cp(out, in_):
            return lambda e: e.tensor_copy(out=out, in_=in_)

        def mm_group(bk, out_ap, items, extra_r=()):
            n = len(items)
            allr = set(extra_r)
            for it in items:
                allr.update(it[2])
            tk = None
            for i, (l, r_, names) in enumerate(items):
                st, sp_ = (i == 0), (i == n - 1)

                def fn(e, l=l, r_=r_, st=st, sp_=sp_):
                    return e.matmul(out_ap, l, r_, start=st, stop=sp_)
                if sp_:
                    tk = P.op("pe", fn, [bf(x) for x in allr], [bank[bk]], signal=True)
                else:
                    P.op("pe", fn, [bf(x) for x in names], [bank[bk]] if st else [], signal=False)
            return tk

        wctr = [0]

        def wblock4(src_ap, ncol=512, cast="act"):
            s_ = wctr[0] % NSLOT
            wctr[0] += 1
            n = 4 * ncol
            stv = stage[s_][:, 0:n].rearrange("p (k c) -> p k c", c=ncol)
            ld("wst%d" % s_, stv, src_ap.rearrange("(k p) c -> p k c", p=128), w=["stage%d" % s_])
            if cast == "act":
                act(A_act(wbf[s_][:, 0:n], stage[s_][:, 0:n], AF.Copy), r=["stage%d" % s_],
                    w=["wbfA%d" % s_, "wbfB%d" % s_])
            else:
                dve(A_cp(wbf[s_][:, 0:n], stage[s_][:, 0:n]), r=["stage%d" % s_],
                    w=["wbfA%d" % s_, "wbfB%d" % s_])
            return s_

        def proj_group(src_ap, K, rhs_fn, rhs_names, ncols=NT, cast="dve"):
            nkb = K // 4
            for kb in range(nkb):
                s_ = wblock4(src_ap[kb * 512:(kb + 1) * 512, :], cast=cast)
                for m in range(4):
                    for k in range(4):
                        kk = kb * 4 + k
                        first, last = (kk == 0), (kk == K - 1)
                        lhsT = wbf[s_][:, k * 512 + m * 128: k * 512 + (m + 1) * 128]
                        rhs = rhs_fn(kk)
                        sig = last or (m == 3 and k == 3)
                        if sig:
                            rd = [bf("wbfA%d" % s_), bf("wbfB%d" % s_)] + [bf(x) for x in rhs_names]
                        else:
                            rd = [bf("wbfA%d" % s_ if k < 2 else "wbfB%d" % s_)] + [bf(x) for x in rhs_names]
                        wr = [bank[m]] if (first or last) else []

                        def fn(e, lhsT=lhsT, rhs=rhs, first=first, last=last, m=m):
                            return e.matmul(psf[m][:, 0:ncols], lhsT, rhs, start=first, stop=last)
                        P.op("pe", fn, rd, wr, signal=sig)

        def proj_tiles(src2d, c0, ntile, K, rhs_fn, rhs_names, evac, ncols=NT, cast="dve"):
            for gg in range(ntile // 4):
                proj_group(src2d[:, c0 + gg * 512: c0 + (gg + 1) * 512], K, rhs_fn, rhs_names, ncols, cast)
                for j in range(4):
                    evac(gg * 4 + j, j)

        def colap(j):
            return cols[:, j:j + 1]

        def reduce_angle(ang, tmp, shape_r, names):
            for kbit in range(11, -1, -1):
                c = 2.0 * PI * (2 ** kbit)
                dve(A_ts(tmp, ang, c, c, ALU.is_ge, ALU.mult), r=names, w=names)
                dve(A_tt(ang, ang, tmp, ALU.subtract), r=names, w=names)

        def cossin(ang_src_fn, cos_out, sin_out, a1, a2, names):
            ang_src_fn(a1, 0.0)
            reduce_angle(a1, a2, None, names)
            dve(A_ts(a1, a1, -PI, None, ALU.add), r=names, w=names)
            act(A_act(sin_out, a1, AF.Sin, scale=1.0), r=names, w=names)
            dve(A_ts(sin_out, sin_out, -1.0, None, ALU.mult), r=names, w=names)
            ang_src_fn(a1, PI / 2)
            reduce_angle(a1, a2, None, names)
            dve(A_ts(a1, a1, -PI, None, ALU.add), r=names, w=names)
            act(A_act(cos_out, a1, AF.Sin, scale=1.0), r=names, w=names)
            dve(A_ts(cos_out, cos_out, -1.0, None, ALU.mult), r=names, w=names)

        ld("cst", consts[:, :], consts_d, w=["consts"])
        act(A_act(ident_bf[:, :], consts[:, 0:128], AF.Copy), r=["consts"], w=["ident"])
        dve(lambda e: e.memset(ones_bf[:, :], 1.0), w=["ones"])
        tril = consts[:, 128:256]
        iotaA = consts[:, 256:288]
        iotaB = consts[:, 288:304]

        def s5slot(i):
            return s5s[:, i * 32:(i + 1) * 32]

        def layer_body(L):
            if nlayers == 1:
                x_src, x_dst = xT, outT
            else:
                x_src = xT if L == 0 else xscr[(L - 1) % 2]
                x_dst = outT if L == nlayers - 1 else xscr[L % 2]

            ld("misc", cols[:, :], cols_d[L], w=["cols"])
            sgw_f = SF[:, 4096:5120]
            ld("misc", sgw_f, sgwT_d[L], w=["SFh1"])
            ld("misc", sgb[:, :], sgb_d[L], w=["sgb"])
            ld("misc", s5a[:, :], s5a_d[L], w=["s5a"])
            for nm_ in ["cols", "SFh1", "sgb", "s5a"]:
                bf(nm_).w = ("misc", P.dsem["misc"])
            for g in range(8):
                dve(A_tt(sgwT_bf[:, g * 128:(g + 1) * 128], sgw_f[:, g * 128:(g + 1) * 128], tril, ALU.mult),
                    r=["SFh1", "consts"], w=["sgwT"])
            for hh in range(2):
                ld("misc2", SF[:, 0:4096], bpad_d[L][:, hh * 4096:(hh + 1) * 4096], w=["SFh0"])
                act(A_act(bpad_bf[:, hh * 4096:(hh + 1) * 4096], SF[:, 0:4096], AF.Copy), r=["SFh0"], w=["bpad"])
            a_re, a_im, ldt = s5a[:, 0:32], s5a[:, 32:64], s5a[:, 64:96]
            AR, DT, MAG, TH, C1, S1, NR, DEN, CFR, CFI, T1, T2, R5R, R5I, T3 = [s5slot(i) for i in range(15)]
            SN = ["s5s"]
            dve(A_ts(AR, a_re, -1e-4, None, ALU.min), r=["s5a"], w=SN)
            act(A_act(DT, ldt, AF.Exp), r=["s5a"], w=SN)
            dve(A_tt(T1, AR, DT, ALU.mult), r=SN, w=SN)
            act(A_act(MAG, T1, AF.Exp), r=SN, w=SN)
            dve(A_tt(TH, a_im, DT, ALU.mult), r=SN + ["s5a"], w=SN)

            def ang_th(dst, off):
                dve(A_ts(dst, TH, 1.0, off, ALU.mult, ALU.add), r=SN, w=SN)
            cossin(ang_th, C1, S1, T1, T2, SN)
            dve(A_tt(NR, MAG, C1, ALU.mult), r=SN, w=SN)
            dve(A_tt(T3, MAG, S1, ALU.mult), r=SN, w=SN)
            dve(A_ts(NR, NR, -1.0, None, ALU.add), r=SN, w=SN)
            dve(A_tt(DEN, AR, AR, ALU.mult), r=SN, w=SN)
            dve(A_tt(T1, a_im, a_im, ALU.mult), r=SN + ["s5a"], w=SN)
            dve(A_tt(DEN, DEN, T1, ALU.add), r=SN, w=SN)
            dve(lambda e: e.reciprocal(out=DEN, in_=DEN), r=SN, w=SN)
            dve(A_tt(T1, NR, AR, ALU.mult), r=SN, w=SN)
            dve(A_tt(T2, T3, a_im, ALU.mult), r=SN + ["s5a"], w=SN)
            dve(A_tt(T1, T1, T2, ALU.add), r=SN, w=SN)
            dve(A_tt(CFR, T1, DEN, ALU.mult), r=SN, w=SN)
            dve(A_tt(T1, T3, AR, ALU.mult), r=SN, w=SN)
            dve(A_tt(T2, NR, a_im, ALU.mult), r=SN + ["s5a"], w=SN)
            dve(A_tt(T1, T1, T2, ALU.subtract), r=SN, w=SN)
            dve(A_tt(CFI, T1, DEN, ALU.mult), r=SN, w=SN)

            def ang_512(dst, off):
                dve(A_ts(dst, TH, float(NT), off, ALU.mult, ALU.add), r=SN, w=SN)
            cossin(ang_512, R5R, R5I, T1, T2, SN)
            TN = ["tabs", "SFh0"]
            tAc = tabA[:, 0:1024].rearrange("p (m a) -> p m a", a=32)
            tAs = tabA[:, 1024:2048].rearrange("p (m a) -> p m a", a=32)
            tBc = tabB[:, 0:512].rearrange("p (m b) -> p m b", b=16)
            tBs = tabB[:, 512:1024].rearrange("p (m b) -> p m b", b=16)
            sA1 = SF[:, 0:1024].rearrange("p (m a) -> p m a", a=32)
            sA2 = SF[:, 1024:2048].rearrange("p (m a) -> p m a", a=32)
            sB1 = SF[:, 2048:2560].rearrange("p (m b) -> p m b", b=16)
            sB2 = SF[:, 2560:3072].rearrange("p (m b) -> p m b", b=16)
            th_bA = TH.unsqueeze(2).to_broadcast([128, 32, 32])
            th_bB = TH.unsqueeze(2).to_broadcast([128, 32, 16])
            io_bA = iotaA.unsqueeze(1).to_broadcast([128, 32, 32])
            io_bB = iotaB.unsqueeze(1).to_broadcast([128, 32, 16])

            def ang_A(dst, off):
                dve(A_tt(dst, th_bA, io_bA, ALU.mult), r=SN + TN + ["consts"], w=TN)
                if off != 0.0:
                    dve(A_ts(dst, dst, off, None, ALU.add), r=TN, w=TN)

            def ang_B(dst, off):
                dve(A_tt(dst, th_bB, io_bB, ALU.mult), r=SN + TN + ["consts"], w=TN)
                if off != 0.0:
                    dve(A_ts(dst, dst, off, None, ALU.add), r=TN, w=TN)
            cossin(ang_A, tAc, tAs, sA1, sA2, TN)
            cossin(ang_B, tBc, tBs, sB1, sB2, TN)
            CN = ["SFh0"]
            for ch in range(4):
                cre = SF[:, 0:1024].rearrange("p (m c) -> p m c", c=128)
                cim = SF[:, 1024:2048].rearrange("p (m c) -> p m c", c=128)
                u1 = SF[:, 2048:3072].rearrange("p (m c) -> p m c", c=128)
                u2 = SF[:, 3072:4096].rearrange("p (m c) -> p m c", c=128)
                ld("misc2", SF[:, 0:1024], cpad_d[L][:, ch * 1024:(ch + 1) * 1024], w=CN)
                ld("misc2", SF[:, 1024:2048], cpad_d[L][:, 4096 + ch * 1024:4096 + (ch + 1) * 1024], w=CN)
                cfr_b = CFR[:, ch * 8:(ch + 1) * 8].unsqueeze(2).to_broadcast([128, 8, 128])
                cfi_b = CFI[:, ch * 8:(ch + 1) * 8].unsqueeze(2).to_broadcast([128, 8, 128])
                cpr = cp_bf[:, ch * 1024:(ch + 1) * 1024].rearrange("p (m c) -> p m c", c=128)
                cpi = cp_bf[:, 4096 + ch * 1024:4096 + (ch + 1) * 1024].rearrange("p (m c) -> p m c", c=128)
                dve(A_tt(u1, cre, cfr_b, ALU.mult), r=CN + SN, w=CN)
                dve(A_tt(u2, cim, cfi_b, ALU.mult), r=CN + SN, w=CN)
                dve(A_tt(cpr, u1, u2, ALU.subtract), r=CN, w=["cp"])
                dve(A_tt(u1, cre, cfi_b, ALU.mult), r=CN + SN, w=CN)
                dve(A_tt(u2, cim, cfr_b, ALU.mult), r=CN + SN, w=CN)
                dve(A_stt(cpi, u1, -1.0, u2, ALU.mult, ALU.subtract), r=CN, w=["cp"])
            dve(lambda e: e.memset(carry[:, :], 0.0), w=["carry"])

            memf = SF[:, 0:4096].rearrange("p (k m) -> p k m", m=NMEM)
            ld("misc2", memf, memT.rearrange("(k p) m -> p k m", p=128), w=["SFh0"])
            msq = SB[:, 0:4096].rearrange("p (k m) -> p k m", m=NMEM)
            act(A_act(msq, memf, AF.Square), r=["SFh0"], w=["SBq0"])
            items = [(ones_bf[:, :], msq[:, k, :], ["ones", "SBq0"]) for k in range(16)]
            mm_group(4, psf[4][:, 0:NMEM], items)
            dve(A_ts(tA[:, 0:NMEM], psf[4][:, 0:NMEM], 1.0 / D, EPS, ALU.mult, ALU.add), r=["bank4"], w=["tA"])
            act(A_act(tA[:, 0:NMEM], tA[:, 0:NMEM], AF.Sqrt), r=["tA"], w=["tA"])
            dve(lambda e: e.reciprocal(out=tA[:, 0:NMEM], in_=tA[:, 0:NMEM]), r=["tA"], w=["tA"])
            memn = SB[:, 4096:8192].rearrange("p (k m) -> p k m", m=NMEM)
            for k in range(16):
                dve(A_stt(memn[:, k, :], memf[:, k, :], colap(32 + k), tA[:, 0:NMEM], ALU.mult, ALU.mult),
                    r=["SFh0", "tA", "cols"], w=["SBq1"])
            def ev_k(mt, bk):
                act(A_act(kT_bf[:, mt * 256:(mt + 1) * 256], psf[bk][:, 0:NMEM], AF.Copy), r=["bank%d" % bk], w=["kT"])
            proj_tiles(wk[L], 0, 8, 16, lambda k: memn[:, k, :], ["SBq1"], ev_k, ncols=NMEM)
            for cg in range(2):
                for kb in range(4):
                    s_ = wblock4(wv[L][kb * 512:(kb + 1) * 512, cg * 512:(cg + 1) * 512], cast="dve")
                    for m2 in range(2):
                        for k in range(4):
                            kk = kb * 4 + k
                            first, last = (kk == 0), (kk == 15)
                            lhsT = memn[:, kk, m2 * 128:(m2 + 1) * 128]
                            rhs = wbf[s_][:, k * 512:(k + 1) * 512]
                            sig = last or (m2 == 1 and k == 3)

                            def fn(e, lhsT=lhsT, rhs=rhs, first=first, last=last, m2=m2):
                                return e.matmul(psf[m2][:, :], lhsT, rhs, start=first, stop=last)
                            P.op("pe", fn, [bf("wbfA%d" % s_), bf("wbfB%d" % s_), bf("SBq1")],
                                 [bank[m2]] if (first or last) else [], signal=sig)
                for m2 in range(2):
                    act(A_act(vtok_bf[:, m2 * 1024 + cg * 512: m2 * 1024 + (cg + 1) * 512], psf[m2][:, :], AF.Copy),
                        r=["bank%d" % m2], w=["vtok"])

            if DEBUG_PHASE == 1:
                raise _Stop()
            for ti in range(ntiles):
                t0 = ti * NT
                hT3 = hT[:, :].rearrange("p (k t) -> p k t", t=NT)
                xf = SF[:, :].rearrange("p (k t) -> p k t", t=NT)
                xsq = SB[:, 0:8192].rearrange("p (k t) -> p k t", t=NT)
                for k in range(16):
                    ld("xin", xf[:, k, :], x_src[k * 128:(k + 1) * 128, t0:t0 + NT], w=["SFh0", "SFh1"])
                for k in range(16):
                    act(A_act(xsq[:, k, :], xf[:, k, :], AF.Square), r=["SFh0", "SFh1"], w=["SBq0", "SBq1"])
                items = [(ones_bf[:, :], xsq[:, k, :], ["ones", "SBq0", "SBq1"]) for k in range(16)]
                mm_group(4, psf[4][:, :], items)
                dve(A_ts(rstd[:, :], psf[4][:, :], 1.0 / D, EPS, ALU.mult, ALU.add), r=["bank4"], w=["rstd"])
                act(A_act(rstd[:, :], rstd[:, :], AF.Sqrt), r=["rstd"], w=["rstd"])
                dve(lambda e: e.reciprocal(out=rstd[:, :], in_=rstd[:, :]), r=["rstd"], w=["rstd"])
                for k in range(16):
                    if k % 4 != 3:
                        dve(A_stt(hT3[:, k, :], xf[:, k, :], colap(k), rstd[:, :], ALU.mult, ALU.mult),
                            r=["SFh0", "SFh1", "rstd", "cols"], w=["hT%d" % k])
                    else:
                        pool(A_ts(tF[:, :], xf[:, k, :], colap(k), None, ALU.mult), r=["SFh0", "SFh1", "cols"], w=["tF"])
                        pool(A_tt(hT3[:, k, :], tF[:, :], rstd[:, :], ALU.mult), r=["tF", "rstd"], w=["hT%d" % k])

                def h_rhs(k):
                    return hT3[:, k, :]

                def win_tiles(c0, ntile, evac, cast="dve"):
                    proj_tiles(w_in[L], c0, ntile, 16, h_rhs, HT, evac, cast=cast)

                if DEBUG_PHASE == 2:
                    raise _Stop()
                vbf = SB[:, 0:4096].rearrange("p (g t) -> p g t", t=NT)
                vsq = SB[:, 4096:8192].rearrange("p (g t) -> p g t", t=NT)
                vtk = SB[:, 8192:12288].rearrange("p (c g d) -> p c g d", g=8, d=128)
                za = SF[:, 0:4096].rearrange("p (g t) -> p g t", t=NT)
                def ev_av(g, bk):
                    act(A_act(vbf[:, g, :], psf[bk][:, :], AF.Copy), r=["bank%d" % bk], w=["SBq0"])
                    act(A_act(vsq[:, g, :], psf[bk][:, :], AF.Square), r=["bank%d" % bk], w=["SBq1"])
                win_tiles(W, 8, ev_av)
                mm_group(4, psf[4][:, :], [(ones_bf[:, :], vbf[:, g, :], ["ones", "SBq0"]) for g in range(8)])
                mm_group(5, psf[5][:, :], [(ones_bf[:, :], vsq[:, g, :], ["ones", "SBq1"]) for g in range(8)])
                act(A_act(tA[:, :], psf[4][:, :], AF.Copy, scale=1.0 / W), r=["bank4"], w=["tA"])
                dve(A_tt(tB[:, :], tA[:, :], tA[:, :], ALU.mult), r=["tA"], w=["tB"])
                dve(A_stt(tB[:, :], psf[5][:, :], 1.0 / W, tB[:, :], ALU.mult, ALU.subtract), r=["bank5", "tB"], w=["tB"])
                dve(A_ts(tB[:, :], tB[:, :], EPS, None, ALU.add), r=["tB"], w=["tB"])
                act(A_act(tB[:, :], tB[:, :], AF.Sqrt), r=["tB"], w=["tB"])
                dve(lambda e: e.reciprocal(out=tB[:, :], in_=tB[:, :]), r=["tB"], w=["tB"])
                VN = ["vn%d" % g for g in range(8)]
                for g in range(8):
                    eng_, tt_, tn_ = (pool, tC, "tC") if g % 2 == 0 else (dve, tD, "tD")
                    eng_(A_tt(tt_[:, :], vbf[:, g, :], tA[:, :], ALU.subtract), r=["SBq0", "tA"], w=[tn_])
                    eng_(A_tt(tt_[:, :], tt_[:, :], tB[:, :], ALU.mult), r=[tn_, "tB"], w=[tn_])
                    eng_(A_ts(vbf[:, g, :], tt_[:, :], colap(48 + g), colap(56 + g), ALU.mult, ALU.add),
                         r=[tn_, "cols"], w=[VN[g]])
                for g in range(8):
                    for c in range(4):
                        P.op("pe", (lambda e, g=g, c=c: e.transpose(pst[:, c * 128:(c + 1) * 128],
                                                                     vbf[:, g, c * 128:(c + 1) * 128], ident_bf[:, :])),
                             [bf(VN[g]), bf("ident")], [bank[7]], signal=(c == 3))
                    pstv = pst[:, 0:512].rearrange("p (c d) -> p c d", d=128)
                    act(A_act(vtk[:, :, g, :], pstv, AF.Copy), r=["bank7"], w=["SBq2"])
                alias(["SBq0"], VN)
                for g in range(8):
                    for c in range(4):
                        P.op("pe", (lambda e, g=g, c=c: e.matmul(psf[6][:, c * 128:(c + 1) * 128], vtk[:, c, g, :],
                                                                  sgwT_bf[:, g * 128:(g + 1) * 128], start=True, stop=True)),
                             [bf("SBq2"), bf("sgwT")], [bank[6]], signal=(c == 3))
                    zv = psf[6][:, :].rearrange("p (c t) -> p c t", t=128)
                    bb = sgb[:, g * 128:(g + 1) * 128].unsqueeze(1).to_broadcast([128, 4, 128])
                    dve(A_tt(za[:, g, :].rearrange("p (c t) -> p c t", t=128), zv, bb, ALU.add),
                        r=["bank6", "sgb"], w=["SFh0"])
                ya3 = ya[:, :].rearrange("p (g t) -> p g t", t=NT)
                def ev_au(g, bk):
                    dve(A_tt(za[:, g, :], psf[bk][:, :], za[:, g, :], ALU.mult), r=["bank%d" % bk, "SFh0"], w=["SFh0"])
                win_tiles(0, 8, ev_au, cast="act")

                def ev_ag(g, bk):
                    tt_, tn_ = (tE, "tE") if g % 2 == 0 else (tF, "tF")
                    act(A_act(tt_[:, :], psf[bk][:, :], AF.Silu), r=["bank%d" % bk], w=[tn_])
                    pool(A_tt(ya3[:, g, :], za[:, g, :], tt_[:, :], ALU.mult), r=["SFh0", tn_], w=["ya"])
                win_tiles(2 * W, 8, ev_ag)

                if DEBUG_PHASE == 3:
                    raise _Stop()
                qT = SB[:, 0:4096].rearrange("p (g t) -> p g t", t=NT)
                ex = SB[:, 4096:5120].rearrange("p (m t) -> p m t", t=NT)
                ox = SF[:, 4096:8192].rearrange("p (g t) -> p g t", t=NT)
                def ev_q(g, bk):
                    act(A_act(qT[:, g, :], psf[bk][:, :], AF.Copy), r=["bank%d" % bk], w=["SBq0"])
                win_tiles(5 * W, 8, ev_q)
                kT3 = kT_bf[:, :].rearrange("p (g m) -> p g m", m=NMEM)
                vt3 = vtok_bf[:, :].rearrange("p (m c) -> p m c", c=1024)
                for hd in range(4):
                    for m2 in range(2):
                        items = [(kT3[:, 2 * hd + dd, m2 * 128:(m2 + 1) * 128], qT[:, 2 * hd + dd, :], ["kT", "SBq0"])
                                 for dd in range(2)]
                        mm_group(4 + m2, psf[4 + m2][:, :], items)
                        act(A_act(ex[:, m2, :], psf[4 + m2][:, :], AF.Exp, scale=1.0 / 16.0),
                            r=["bank%d" % (4 + m2)], w=["SBq1"])
                    mm_group(6, psf[6][:, :], [(ones_bf[:, :], ex[:, m2, :], ["ones", "SBq1"]) for m2 in range(2)])
                    dve(lambda e: e.reciprocal(out=tA[:, :], in_=psf[6][:, :]), r=["bank6"], w=["tA"])
                    for dd in range(2):
                        ch = 2 * hd + dd
                        items = [(vt3[:, m2, ch * 128:(ch + 1) * 128], ex[:, m2, :], ["vtok", "SBq1"])
                                 for m2 in range(2)]
                        mm_group(6, psf[6][:, :], items)
                        dve(A_tt(ox[:, ch, :], psf[6][:, :], tA[:, :], ALU.mult), r=["bank6", "tA"], w=["SFh1"])
                yx3 = yx[:, :].rearrange("p (g t) -> p g t", t=NT)
                def ev_xg(g, bk):
                    tt_, tn_ = (tE, "tE") if g % 2 == 0 else (tF, "tF")
                    act(A_act(tt_[:, :], psf[bk][:, :], AF.Silu), r=["bank%d" % bk], w=[tn_])
                    pool(A_tt(yx3[:, g, :], ox[:, g, :], tt_[:, :], ALU.mult), r=["SFh1", tn_], w=["yx"])
                win_tiles(6 * W, 8, ev_xg)

                if DEBUG_PHASE == 4:
                    raise _Stop()
                uT = SB[:, 0:4096].rearrange("p (g t) -> p g t", t=NT)
                ypre = SF[:, 0:4096].rearrange("p (g t) -> p g t", t=NT)
                hre = SB[:, 4096:4608]
                him = SB[:, 4608:5120]
                bp4 = bpad_bf[:, :].rearrange("p (r m c) -> p r m c", r=2, c=128)
                cp4 = cp_bf[:, :].rearrange("p (r m c) -> p r m c", r=2, c=128)
                tAc4 = tabA[:, :].rearrange("p (r m a) -> p r m a", r=2, a=32)
                tBc4 = tabB[:, :].rearrange("p (r m b) -> p r m b", r=2, b=16)
                alias(S5T, ["SFh1"])
                alias(["hre", "him"], ["SBq1"])
                def ev_bx(g, bk):
                    act(A_act(uT[:, g, :], psf[bk][:, :], AF.Copy), r=["bank%d" % bk], w=["SBq0"])
                win_tiles(3 * W, 8, ev_bx)

                def v3(tt_):
                    return tt_[:, :].rearrange("p (a b) -> p a b", b=16)
                gre, gim = SF[:, 4096:4608], SF[:, 4608:5120]
                Gre, Gim = SF[:, 5120:5632], SF[:, 5632:6144]
                m1, m2_ = SF[:, 6144:6656], SF[:, 6656:7168]
                p1, p2 = SF[:, 7168:7680], SF[:, 7680:8192]
                TABS = [(tA, tB, "tA", "tB"), (tC, tD, "tC", "tD")]

                def emit_tables(m):
                    cT_, sT_, cn_, sn_ = TABS[m % 2]
                    cA = tAc4[:, 0, m, :].unsqueeze(2).to_broadcast([128, 32, 16])
                    sA = tAc4[:, 1, m, :].unsqueeze(2).to_broadcast([128, 32, 16])
                    cB = tBc4[:, 0, m, :].unsqueeze(1).to_broadcast([128, 32, 16])
                    sB = tBc4[:, 1, m, :].unsqueeze(1).to_broadcast([128, 32, 16])
                    pool(A_tt(v3(tE), cA, cB, ALU.mult), r=["tabs"], w=["tE"])
                    pool(A_tt(v3(tF), sA, sB, ALU.mult), r=["tabs"], w=["tF"])
                    pool(A_tt(cT_[:, :], tE[:, :], tF[:, :], ALU.subtract), r=["tE", "tF"], w=[cn_])
                    pool(A_tt(v3(tE), sA, cB, ALU.mult), r=["tabs"], w=["tE"])
                    dve(A_tt(v3(tF), cA, sB, ALU.mult), r=["tabs"], w=["tF"])
                    dve(A_tt(sT_[:, :], tE[:, :], tF[:, :], ALU.add), r=["tE", "tF"], w=[sn_])

                emit_tables(0)
                for q in range(8):
                    for mm_ in range(4):
                        m = 4 * q + mm_
                        cosT, sinT, cn_, sn_ = TABS[m % 2]
                        mm_group(4, psf[4][:, :], [(bp4[:, 0, m, :], uT[:, q, :], ["bpad", "SBq0"])])
                        mm_group(5, psf[5][:, :], [(bp4[:, 1, m, :], uT[:, q, :], ["bpad", "SBq0"])])
                        dve(A_tt(m1, psf[4][:, :], cosT[:, :], ALU.mult), r=["bank4", cn_], w=["m1"])
                        dve(A_tt(m2_, psf[5][:, :], sinT[:, :], ALU.mult), r=["bank5", sn_], w=["m2"])
                        dve(A_tt(gre, m1, m2_, ALU.add), r=["m1", "m2"], w=["gre"])
                        dve(A_tt(m1, psf[5][:, :], cosT[:, :], ALU.mult), r=["bank5", cn_], w=["m1"])
                        dve(A_tt(m2_, psf[4][:, :], sinT[:, :], ALU.mult), r=["bank4", sn_], w=["m2"])
                        dve(A_tt(gim, m1, m2_, ALU.subtract), r=["m1", "m2"], w=["gim"])
                        rb = s5slot(2)[:, m:m + 1].to_broadcast([128, NT])
                        cre_i = carry[:, m:m + 1]
                        cim_i = carry[:, 32 + m:33 + m]
                        dve((lambda e, rb=rb, cre_i=cre_i: e.tensor_tensor_scan(out=Gre, data0=rb, data1=gre, initial=cre_i,
                                                                                op0=ALU.mult, op1=ALU.add)),
                            r=["gre", "s5s", "carry"], w=["Gre"])
                        dve((lambda e, rb=rb, cim_i=cim_i: e.tensor_tensor_scan(out=Gim, data0=rb, data1=gim, initial=cim_i,
                                                                                op0=ALU.mult, op1=ALU.add)),
                            r=["gim", "s5s", "carry"], w=["Gim"])
                        r5r, r5i = s5slot(12)[:, m:m + 1], s5slot(13)[:, m:m + 1]
                        gl_r, gl_i = Gre[:, NT - 1:NT], Gim[:, NT - 1:NT]
                        tmpc = s5slot(15)[:, m:m + 1]
                        dve(A_tt(tmpc, gl_i, r5i, ALU.mult), r=["Gim", "s5s"], w=["tmpc"])
                        dve(A_stt(cre_i, gl_r, r5r, tmpc, ALU.mult, ALU.subtract), r=["Gre", "tmpc", "s5s"], w=["carry"])
                        dve(A_tt(tmpc, gl_i, r5r, ALU.mult), r=["Gim", "s5s"], w=["tmpc"])
                        dve(A_stt(cim_i, gl_r, r5i, tmpc, ALU.mult, ALU.add), r=["Gre", "tmpc", "s5s"], w=["carry"])
                        if m + 1 < 32:
                            emit_tables(m + 1)
                        pool(A_tt(p1, Gre, cosT[:, :], ALU.mult), r=["Gre", cn_], w=["p1"])
                        pool(A_tt(p2, Gim, sinT[:, :], ALU.mult), r=["Gim", sn_], w=["p2"])
                        pool(A_tt(hre, p1, p2, ALU.subtract), r=["p1", "p2"], w=["hre"])
                        dve(A_tt(m1, Gre, sinT[:, :], ALU.mult), r=["Gre", sn_], w=["m1"])
                        dve(A_tt(m2_, Gim, cosT[:, :], ALU.mult), r=["Gim", cn_], w=["m2"])
                        dve(A_tt(him, m1, m2_, ALU.add), r=["m1", "m2"], w=["him"])
                        st = (mm_ == 0)
                        last = (mm_ == 3)
                        P.op("pe", (lambda e, m=m, st=st: e.matmul(psf[6][:, :], cp4[:, 0, m, :], hre, start=st, stop=False)),
                             [bf("cp"), bf("hre")], [bank[6]] if st else [], signal=True)
                        P.op("pe", (lambda e, m=m, last=last: e.matmul(psf[6][:, :], cp4[:, 1, m, :], him, start=False, stop=last)),
                             [bf("cp"), bf("him")], [bank[6]] if last else [], signal=True)
                    dve(A_stt(ypre[:, q, :], uT[:, q, :], colap(72 + q), psf[6][:, :], ALU.mult, ALU.add),
                        r=["bank6", "SBq0", "cols"], w=["SFh0"])
                alias(["SFh1"], S5T)
                alias(["SBq1"], ["hre", "him"])
                ygl = SB[:, 8192:12288].rearrange("p (g t) -> p g t", t=NT)
                YP = ["yp%d" % q for q in range(8)]
                for q in range(8):
                    eng_, tt_, tn_ = (pool, tC, "tC") if q % 2 == 0 else (dve, tD, "tD")
                    eng_(A_tt(tt_[:, :], ypre[:, q, :], ypre[:, q, :], ALU.mult), r=["SFh0"], w=[tn_])
                    eng_(A_ts(tt_[:, :], tt_[:, :], 0.044715, 1.0, ALU.mult, ALU.add), r=[tn_], w=[tn_])
                    eng_(A_tt(tt_[:, :], tt_[:, :], ypre[:, q, :], ALU.mult), r=[tn_, "SFh0"], w=[tn_])
                    act(A_act(tt_[:, :], tt_[:, :], AF.Sigmoid, scale=1.5957691216057308), r=[tn_], w=[tn_])
                    eng_(A_tt(ypre[:, q, :], ypre[:, q, :], tt_[:, :], ALU.mult), r=[tn_, "SFh0"], w=[YP[q]])
                    act(A_act(ygl[:, q, :], ypre[:, q, :], AF.Copy), r=[YP[q]], w=["SBq2"])
                yb3 = yb[:, :].rearrange("p (g t) -> p g t", t=NT)
                def ev_glu(g, bk):
                    tt_, tn_ = (tE, "tE") if g % 2 == 0 else (tF, "tF")
                    act(A_act(tt_[:, :], psf[bk][:, :], AF.Sigmoid, bias=colap(64 + g)), r=["bank%d" % bk, "cols"], w=[tn_])
                    pool(A_tt(yb3[:, g, :], ypre[:, g, :], tt_[:, :], ALU.mult), r=[YP[g], tn_], w=["yb"])
                proj_tiles(glu_w[L], 0, 8, 8, lambda k: ygl[:, k, :], ["SBq2"], ev_glu)
                alias(["SFh0"], YP)

                def ev_bg(g, bk):
                    tt_, tn_ = (tE, "tE") if g % 2 == 0 else (tF, "tF")
                    act(A_act(tt_[:, :], psf[bk][:, :], AF.Silu), r=["bank%d" % bk], w=[tn_])
                    pool(A_tt(yb3[:, g, :], yb3[:, g, :], tt_[:, :], ALU.mult), r=["yb", tn_], w=["yb"])
                win_tiles(4 * W, 8, ev_bg)

                if DEBUG_PHASE == 5:
                    raise _Stop()
                mg = SB[:, 8192:16384].rearrange("p (k t) -> p k t", t=NT)
                MG = ["SBq2", "SBq3"]
                ys = [ya3, yb3, yx3]
                ynm = ["ya", "yb", "yx"]
                GT = ["gt%d" % j for j in range(4)]
                AC = ["acc%d" % j for j in range(4)]
                alias(GT + AC + ["mtmp0", "mtmp1"], ["SFh0", "SFh1"])
                gts = [SF[:, j * NT:(j + 1) * NT] for j in range(4)]
                accs = [SF[:, (4 + j) * NT:(5 + j) * NT] for j in range(4)]
                mtmp = [SF[:, (8 + j) * NT:(9 + j) * NT] for j in range(2)]
                for mq in range(4):
                    for n in range(3):
                        def ev_gate(g, bk):
                            act(A_act(gts[bk], psf[bk][:, :], AF.Sigmoid), r=["bank%d" % bk], w=[GT[bk]])
                        proj_tiles(w_in[L], 7 * W + n * D + mq * 512, 4, 16, h_rhs, HT, ev_gate)

                        def ev_br(g, bk, n=n, mq=mq):
                            if n == 0:
                                dve(A_tt(accs[bk], psf[bk][:, :], gts[bk], ALU.mult), r=["bank%d" % bk, GT[bk]], w=[AC[bk]])
                            else:
                                tm = mtmp[bk % 2]
                                tmn = "mtmp%d" % (bk % 2)
                                dve(A_tt(tm, psf[bk][:, :], gts[bk], ALU.mult), r=["bank%d" % bk, GT[bk]], w=[tmn])
                                if n == 1:
                                    dve(A_tt(accs[bk], accs[bk], tm, ALU.add), r=[AC[bk], tmn], w=[AC[bk]])
                                else:
                                    dve(A_tt(mg[:, mq * 4 + bk, :], accs[bk], tm, ALU.add), r=[AC[bk], tmn], w=MG)
                        proj_tiles(w_br[L][n], mq * 512, 4, 8, (lambda k, n=n: ys[n][:, k, :]), [ynm[n]], ev_br,
                                   cast="act")
                alias(["SFh0", "SFh1"], GT + AC + ["mtmp0", "mtmp1"])

                if DEBUG_PHASE in (6, 10):
                    raise _Stop()
                of = SF[:, :].rearrange("p (k t) -> p k t", t=NT)
                osq = SB[:, 0:8192].rearrange("p (k t) -> p k t", t=NT)
                SFN = ["SFh0", "SFh1"]
                def ev_out(mt, bk):
                    dve(A_cp(of[:, mt, :], psf[bk][:, :]), r=["bank%d" % bk], w=SFN)
                    act(A_act(osq[:, mt, :], of[:, mt, :], AF.Square), r=SFN, w=["SBq0", "SBq1"])
                XB = [(tC, "tC"), (tD, "tD"), (tE, "tE"), (tF, "tF")]

                def ld_x(mt):
                    xb_, xn = XB[mt % 4]
                    ld("xin2_%d" % (mt % 4), xb_[:, :], x_src[mt * 128:(mt + 1) * 128, t0:t0 + NT], w=[xn])
                for mt in range(4):
                    ld_x(mt)
                proj_tiles(w_out[L], 0, 16, 16, lambda k: mg[:, k, :], MG, ev_out, cast="act")
                if DEBUG_PHASE in (7, 71, 72, 73):
                    raise _Stop()
                mm_group(4, psf[4][:, :], [(ones_bf[:, :], osq[:, k, :], ["ones", "SBq0", "SBq1"]) for k in range(16)])
                dve(A_ts(rstd[:, :], psf[4][:, :], 1.0 / D, EPS, ALU.mult, ALU.add), r=["bank4"], w=["rstd"])
                act(A_act(rstd[:, :], rstd[:, :], AF.Sqrt), r=["rstd"], w=["rstd"])
                dve(lambda e: e.reciprocal(out=rstd[:, :], in_=rstd[:, :]), r=["rstd"], w=["rstd"])
                if DEBUG_PHASE == 8:
                    raise _Stop()
                for mt in range(16):
                    xb_, xn = XB[mt % 4]
                    dve(A_stt(of[:, mt, :], of[:, mt, :], colap(16 + mt), rstd[:, :], ALU.mult, ALU.mult),
                        r=SFN + ["rstd", "cols"], w=SFN)
                    dve(A_tt(xb_[:, :], xb_[:, :], of[:, mt, :], ALU.add), r=[xn] + SFN, w=[xn])
                    if DEBUG_PHASE == 9:
                        continue
                    P.dma("xout_%d" % (mt % 4),
                          (lambda e, xb_=xb_, mt=mt, t0=t0, x_dst=x_dst: e.dma_start(
                              out=x_dst[mt * 128:(mt + 1) * 128, t0:t0 + NT], in_=xb_[:, :])),
                          [bf(xn)], [])
                    if mt + 4 < 16:
                        ld_x(mt + 4)
            lw = [(s_, v_) for s_, v_ in P.dsem.items() if s_.startswith("xout_")]
            P.ops["sp"].append((lw, None, None, 0))

        try:
            for L_ in range(nlayers):
                layer_body(L_)
        except _Stop:
            if DEBUG_PHASE == 72:
                dbg = nc.dram_tensor("dbg", [128, 56 * NT], F32, kind="ExternalOutput").ap()
                srcs = ([(hT[:, k * NT:(k + 1) * NT], "hT%d" % k) for k in range(16)]
                        + [(ya[:, k * NT:(k + 1) * NT], "ya") for k in range(8)]
                        + [(yb[:, k * NT:(k + 1) * NT], "yb") for k in range(8)]
                        + [(yx[:, k * NT:(k + 1) * NT], "yx") for k in range(8)]
                        + [(SB[:, 8192 + k * NT: 8192 + (k + 1) * NT], "SBq2" if k < 8 else "SBq3") for k in range(16)])
                for i, (src, nm) in enumerate(srcs):
                    tt_, tn_ = [(tC, "tC"), (tD, "tD"), (tE, "tE"), (tF, "tF")][i % 4]
                    dve(A_cp(tt_[:, :], src), r=[nm], w=[tn_])
                    P.dma("xout_%d" % (i % 2), (lambda e, tt_=tt_, i=i: e.dma_start(out=dbg[:, i * NT:(i + 1) * NT], in_=tt_[:, :])),
                          [bf(tn_)], [])
                for k in range(16):
                    P.dma("xout_%d" % (k % 2), (lambda e, k=k: e.dma_start(out=outT[k * 128:(k + 1) * 128, 0:NT],
                                                                         in_=SF[:, k * NT:(k + 1) * NT])),
                          [bf("SFh0"), bf("SFh1")], [])

        final_waits = [(s, v) for s, v in P.dsem.items() if s.startswith("xout_")]
        P.ops["sp"].append((final_waits, None, None, 0))

        semnames = set(["pe", "act", "dve", "pool"]) | set(P.dsem.keys())
        sems = {n: sem("s_" + n) for n in sorted(semnames)}
        with nc.Block() as block:
            def replay(engname):
                def f(e):
                    for waits, fn, sname, inc in P.ops[engname]:
                        for (s, v) in waits:
                            e.wait_ge(sems[s], v)
                        if fn is None:
                            continue
                        inst = fn(e)
                        if sname is not None:
                            inst.then_inc(sems[sname], inc)
                return f
            block.tensor(replay("pe"))
            block.scalar(replay("act"))
            block.vector(replay("dve"))
            block.gpsimd(replay("pool"))
            block.sync(replay("sp"))
    return nc


def _consts():
    c = np.zeros((128, 304), np.float32)
    c[:, 0:128] = np.eye(128, dtype=np.float32)
    s = np.arange(128)[:, None]
    t = np.arange(128)[None, :]
    c[:, 128:256] = (s <= t).astype(np.float32)
    c[:, 256:288] = (16.0 * np.arange(32, dtype=np.float32))[None, :]
    c[:, 288:304] = np.arange(16, dtype=np.float32)[None, :]
    return c


def _layer_layouts(inp, layers):
    f = np.float32
    out = {}

    def colmaj(v, n):
        return np.ascontiguousarray(v.reshape(n, 128).T)

    cols, sgwT, sgb, s5a, bpad, cpad = [], [], [], [], [], []
    for l in layers:
        c = np.concatenate([
            colmaj(inp["pre_norm_g"][l], 16), colmaj(inp["post_norm_g"][l], 16), colmaj(inp["mem_norm_g"][l], 16),
            colmaj(inp["sg_ln_g"][l], 8), colmaj(inp["sg_ln_b"][l], 8), colmaj(inp["glu_b"][l], 8),
            colmaj(inp["ssm_d"][l].reshape(-1), 8)], axis=1).astype(f)
        cols.append(c)
        sgwT.append(np.ascontiguousarray(inp["sg_w"][l].transpose(2, 0, 1)).reshape(128, 8 * 128).astype(f))
        sgb.append(np.ascontiguousarray(np.broadcast_to(inp["sg_b"][l].reshape(1, 8 * 128), (128, 8 * 128))).astype(f))

        def pairlay(a):
            return a.reshape(32, 2, 64).transpose(1, 2, 0).reshape(128, 32)
        ldt = np.broadcast_to(inp["ssm_log_dt"][l][:, None], (64, 64))
        s5a.append(np.concatenate([pairlay(inp["ssm_a_re"][l]), pairlay(inp["ssm_a_im"][l]), pairlay(ldt)], axis=1).astype(f))
        bp = np.zeros((2, 32, 128, 128), f)
        cp = np.zeros((2, 32, 128, 128), f)
        for ri, (bsrc, csrc) in enumerate([(inp["ssm_b_re"][l], inp["ssm_c_re"][l]), (inp["ssm_b_im"][l], inp["ssm_c_im"][l])]):
            for m in range(32):
                for s in range(2):
                    g = 2 * m + s
                    gl = g % 8
                    bp[ri, m, gl * 16:(gl + 1) * 16, s * 64:(s + 1) * 64] = bsrc[g].T
                    cp[ri, m, s * 64:(s + 1) * 64, gl * 16:(gl + 1) * 16] = csrc[g].T
        bpad.append(np.ascontiguousarray(bp.transpose(2, 0, 1, 3)).reshape(128, 2 * 32 * 128))
        cpad.append(np.ascontiguousarray(cp.transpose(2, 0, 1, 3)).reshape(128, 2 * 32 * 128))
    out["cols"] = np.stack(cols)
    out["sgwT"] = np.stack(sgwT)
    out["sgb"] = np.stack(sgb)
    out["s5a"] = np.stack(s5a)
    out["bpad"] = np.stack(bpad)
    out["cpad"] = np.stack(cpad)
    for k_src, k_dst in [("w_in", "w_in"), ("glu_w", "glu_w"), ("xa_wk", "wk"), ("xa_wv", "wv"),
                         ("w_branch", "w_br"), ("w_out", "w_out")]:
        out[k_dst] = np.ascontiguousarray(inp[k_src][list(layers)]).astype(f)
    out["consts"] = _consts()
    return out


_PROG_CACHE = {}


def _get_prog(ntiles, nlayers):
    key = (ntiles, nlayers)
    if key not in _PROG_CACHE:
        _PROG_CACHE[key] = build_program(ntiles, nlayers)
    return _PROG_CACHE[key]


FUSED = True
NCORES = 4


def run_layers(x, mem, inp, layers, ntiles):
    nb = x.shape[0]
    lay = _layer_layouts(inp, layers)
    nc = _get_prog(ntiles, len(layers))
    in_maps = []
    for b in range(nb):
        m = dict(lay)
        m["xT"] = np.ascontiguousarray(x[b].T)
        m["memT"] = np.ascontiguousarray(mem[b].T)
        in_maps.append(m)
    res = run_bass_kernel_spmd(nc, in_maps, core_ids=list(range(nb)))
    return np.stack([np.ascontiguousarray(res.results[b]["outT"].T) for b in range(nb)])


def kernel(**inputs):
    inp = {k: np.asarray(v) for k, v in inputs.items()}
    x = inp["x"].astype(np.float32)
    mem = inp["mem"].astype(np.float32)
    ntiles = x.shape[1] // NT
    if FUSED:
        return run_layers(x, mem, inp, list(range(DEPTH)), ntiles).astype(np.float32)
    for l in range(DEPTH):
        x = run_layers(x, mem, inp, [l], ntiles)
    return x.astype(np.float32)
```
